# Optimizing a Trainium2 kernel written in Bass

```python
import math
import jax, jax.numpy as jnp
from jax import lax
import numpy as np

D_MODEL = 1024
BATCH = 4
SEQ = 4096
DEPTH = 2

N_A = DEPTH // 2
N_B = DEPTH - N_A

D_FF = ((8 * D_MODEL // 3 + 127) // 128) * 128
FFN_RES_SCALE = 0.5

D_A = 2 * D_MODEL
CHUNK = 128
SGU_GROUPS = 8
SGU_GROUP_DIM = D_A // SGU_GROUPS

SB_HEADS = 16
SB_HEAD_DIM = D_MODEL // SB_HEADS
SB_KV_HEADS = 4
SB_GROUP = SB_HEADS // SB_KV_HEADS
Q_BLOCK = 128

NORM_EPS = 1e-6
LN_EPS = 1e-5

kernel_name = "yoco_gmlp_stickbreak_macaron"


def rms_norm(x, g):
    xf = x.astype(jnp.float32)
    y = xf * lax.rsqrt(jnp.mean(xf * xf, axis=-1, keepdims=True) + NORM_EPS)
    return (y * g.astype(jnp.float32)).astype(x.dtype)


def layer_norm(x, g, b):
    xf = x.astype(jnp.float32)
    mu = jnp.mean(xf, axis=-1, keepdims=True)
    xc = xf - mu
    var = jnp.mean(xc * xc, axis=-1, keepdims=True)
    y = xc * lax.rsqrt(var + LN_EPS) * g.astype(jnp.float32) + b.astype(jnp.float32)
    return y.astype(x.dtype)


def swiglu(h, w_in, w_out):
    gu = h @ w_in
    gate, up = jnp.split(gu, 2, axis=-1)
    return (jax.nn.silu(gate) * up) @ w_out


def chunked_sgu(h, w_in, ln_g, ln_b, w_s, b_s, w_out):
    B, S, _ = h.shape
    z = jax.nn.gelu(h @ w_in)
    u, v = jnp.split(z, 2, axis=-1)
    v = layer_norm(v, ln_g, ln_b)
    v = v.reshape(B, S // CHUNK, CHUNK, SGU_GROUPS, SGU_GROUP_DIM)
    causal = jnp.tril(jnp.ones((CHUNK, CHUNK), dtype=w_s.dtype))
    ws = w_s * causal[None]
    v = jnp.einsum('gts,bnsgc->bntgc', ws, v) + b_s.T[None, None, :, :, None]
    v = v.reshape(B, S, D_A)
    return (u * v) @ w_out


def stick_breaking_attention(q, k, v):
    B, S = q.shape[0], q.shape[1]
    nb = S // Q_BLOCK
    scale = 1.0 / math.sqrt(SB_HEAD_DIM)
    qf = (q.astype(jnp.float32) * scale).reshape(B, nb, Q_BLOCK, SB_KV_HEADS, SB_GROUP, SB_HEAD_DIM)
    qf = jnp.moveaxis(qf, 1, 0)
    kf = k.astype(jnp.float32)
    vf = v.astype(jnp.float32)
    key_pos = jnp.arange(S)

    def one_block(args):
        qb, bi = args
        q_pos = bi * Q_BLOCK + jnp.arange(Q_BLOCK)
        mask = key_pos[None, :] < q_pos[:, None]
        z = jnp.einsum('bqkgd,bskd->bkgqs', qb, kf)
        log_1m_beta = jnp.where(mask, jax.nn.log_sigmoid(-z), 0.0)
        rev = lax.cumsum(log_1m_beta, axis=4, reverse=True)
        excl = jnp.concatenate([rev[..., 1:], jnp.zeros_like(rev[..., :1])], axis=-1)
        a = jnp.where(mask, jnp.exp(jax.nn.log_sigmoid(z) + excl), 0.0)
        o = jnp.einsum('bkgqs,bskd->bqkgd', a, vf)
        return o.reshape(B, Q_BLOCK, SB_HEADS * SB_HEAD_DIM)

    o = lax.map(one_block, (qf, jnp.arange(nb)))
    o = jnp.moveaxis(o, 0, 1).reshape(B, S, SB_HEADS * SB_HEAD_DIM)
    return o.astype(q.dtype)


def setup_inputs(seed: int = 0) -> dict:
    key = jax.random.key(seed)
    ks = jax.random.split(key, 20)
    f32 = jnp.float32
    nrm = lambda k, shape, fan_in: jax.random.normal(k, shape, f32) * (fan_in ** -0.5)
    gain = lambda k, shape: 1.0 + 0.02 * jax.random.normal(k, shape, f32)
    HD = SB_HEADS * SB_HEAD_DIM
    KVD = SB_KV_HEADS * SB_HEAD_DIM
    return {
        "x": jax.random.normal(ks[0], (BATCH, SEQ, D_MODEL), f32),
        "ffn_norm": gain(ks[1], (DEPTH, 2, D_MODEL)),
        "ffn_w_in": nrm(ks[2], (DEPTH, 2, D_MODEL, 2 * D_FF), D_MODEL),
        "ffn_w_out": nrm(ks[3], (DEPTH, 2, D_FF, D_MODEL), D_FF),
        "mix_norm": gain(ks[4], (DEPTH, D_MODEL)),
        "a_w_in": nrm(ks[5], (N_A, D_MODEL, 2 * D_A), D_MODEL),
        "a_ln_g": gain(ks[6], (N_A, D_A)),
        "a_ln_b": 0.02 * jax.random.normal(ks[7], (N_A, D_A), f32),
        "a_w_s": 0.5 * nrm(ks[8], (N_A, SGU_GROUPS, CHUNK, CHUNK), CHUNK),
        "a_b_s": gain(ks[9], (N_A, SGU_GROUPS, CHUNK)),
        "a_w_out": nrm(ks[10], (N_A, D_A, D_MODEL), D_A),
        "kv_norm": gain(ks[11], (D_MODEL,)),
        "w_kv": nrm(ks[12], (D_MODEL, 2 * KVD), D_MODEL),
        "b_w_q": nrm(ks[13], (N_B, D_MODEL, HD), D_MODEL),
        "b_w_o": nrm(ks[14], (N_B, HD, D_MODEL), HD),
        "final_norm": gain(ks[15], (D_MODEL,)),
    }


def reference(x, ffn_norm, ffn_w_in, ffn_w_out, mix_norm, a_w_in, a_ln_g, a_ln_b, a_w_s,
              a_b_s, a_w_out, kv_norm, w_kv, b_w_q, b_w_o, final_norm):
    B, S, _ = x.shape
    k_shared = None
    v_shared = None
    for i in range(DEPTH):
        if i == N_A:
            kv = rms_norm(x, kv_norm) @ w_kv
            k_flat, v_flat = jnp.split(kv, 2, axis=-1)
            k_shared = k_flat.reshape(B, S, SB_KV_HEADS, SB_HEAD_DIM)
            v_shared = v_flat.reshape(B, S, SB_KV_HEADS, SB_HEAD_DIM)
        x = x + FFN_RES_SCALE * swiglu(rms_norm(x, ffn_norm[i, 0]), ffn_w_in[i, 0], ffn_w_out[i, 0])
        h = rms_norm(x, mix_norm[i])
        if i < N_A:
            x = x + chunked_sgu(h, a_w_in[i], a_ln_g[i], a_ln_b[i], a_w_s[i], a_b_s[i], a_w_out[i])
        else:
            j = i - N_A
            q = (h @ b_w_q[j]).reshape(B, S, SB_HEADS, SB_HEAD_DIM)
            x = x + stick_breaking_attention(q, k_shared, v_shared) @ b_w_o[j]
        x = x + FFN_RES_SCALE * swiglu(rms_norm(x, ffn_norm[i, 1]), ffn_w_in[i, 1], ffn_w_out[i, 1])
    return rms_norm(x, final_norm)
```

```python
import contextlib
import numpy as np
import concourse.bass as bass
import concourse.mybir as mybir
from concourse.bass_utils import run_bass_kernel_spmd

F32 = mybir.dt.float32
BF16 = mybir.dt.bfloat16
AF = mybir.ActivationFunctionType
ALU = mybir.AluOpType

D = 1024
DFF = 2816
NFC = 22
S = 4096
NCH = 16
TOK = NCH * 128
NORM_EPS = 1e-6
LN_EPS = 1e-5


def chunk_owner(gb):
    m, r = divmod(gb, 4)
    if r == 0:
        return 0, 2 * m
    if r == 3:
        return 0, 2 * m + 1
    if r == 1:
        return 1, 2 * m
    return 1, 2 * m + 1


def local_to_global(h, j):
    m, r = divmod(j, 2)
    if h == 0:
        return 4 * m + (0 if r == 0 else 3)
    return 4 * m + (1 if r == 0 else 2)


class Buf:
    __slots__ = ("name", "w", "r", "dsem")

    def __init__(self, name):
        self.name = name
        self.w = None
        self.r = {}
        self.dsem = None


class EngS:
    def __init__(self, key, eng, sem):
        self.key = key
        self.eng = eng
        self.sem = sem
        self.count = 0
        self.seen = {}


class Tracker:
    def __init__(self, nc, es, n_dma_sems=40):
        self.nc = nc
        self.E = {}
        for key, eng in (("pe", nc.tensor), ("act", nc.scalar), ("dve", nc.vector),
                         ("pool", nc.gpsimd), ("sp", nc.sync)):
            self.E[key] = EngS(key, eng, es.enter_context(nc.semaphore("s_" + key)))
        self.dpool = [["d%d" % i, es.enter_context(nc.semaphore("s_d%d" % i)), 0] for i in range(n_dma_sems)]
        self.dfree = {"pool": list(range(0, 18)), "sp": list(range(18, 38)), "cc": list(range(38, n_dma_sems))}
        self.downer = {}
        self.dused = {}
        self.bufs = []

    def buf(self, name):
        b = Buf(name)
        self.bufs.append(b)
        return b

    def bufs_n(self, name, n):
        return [self.buf("%s%d" % (name, i)) for i in range(n)]

    def _deps(self, reads, writes):
        deps = {}

        def add(tok):
            if tok is None:
                return
            k, s, v = tok
            if k not in deps or deps[k][1] < v:
                deps[k] = (s, v)
        for b in reads:
            add(b.w)
        for b in writes:
            add(b.w)
            for t in b.r.values():
                add(t)
        return deps

    def _wait(self, e, deps):
        for k, (s, v) in deps.items():
            if e.key == "pe" and k == "pe":
                continue
            if e.seen.get(k, 0) >= v:
                continue
            e.eng.wait_ge(s, v)
            e.seen[k] = v

    def _mark(self, tok, reads, writes):
        k = tok[0]
        for b in reads:
            b.r[k] = tok
        for b in writes:
            b.w = tok
            b.r = {}

    def op(self, en, fn, reads=(), writes=()):
        e = self.E[en]
        self._wait(e, self._deps(reads, writes))
        inst = fn(e.eng)
        e.count += 1
        inst.then_inc(e.sem, 1)
        tok = (e.key, e.sem, e.count)
        self._mark(tok, reads, writes)
        return tok

    def dma(self, en, out, in_, reads=(), writes=(), sembuf=None, fn=None, inc=16):
        e = self.E[en]
        self._wait(e, self._deps(reads, writes))
        sb = sembuf if sembuf is not None else (writes[0] if writes else reads[0])
        if sb.dsem is None:
            cls = "cc" if fn is not None else en
            sb.dsem = self.dfree[cls].pop()
            self.downer[sb.dsem] = cls
            self.dused[id(sb)] = sb
        ent = self.dpool[sb.dsem]
        inst = fn(e.eng) if fn is not None else e.eng.dma_start(out=out, in_=in_)
        ent[2] += inc
        inst.then_inc(ent[1], inc)
        tok = (ent[0], ent[1], ent[2])
        self._mark(tok, reads, writes)
        return tok

    def barrier(self):
        deps = {}
        for e in self.E.values():
            if e.count:
                deps[e.key] = (e.sem, e.count)
        for ent in self.dpool:
            if ent[2]:
                deps[ent[0]] = (ent[1], ent[2])
        for e in self.E.values():
            d = {k: v for k, v in deps.items() if k != e.key}
            self._wait(e, d)
        for b in self.bufs:
            b.w = None
            b.r = {}
            if b.dsem is not None:
                self.dfree[self.downer[b.dsem]].append(b.dsem)
                b.dsem = None
        self.dused = {}
        self.bufs = []


class Arena:
    def __init__(self, ap_f32, total_f32):
        self.ap = ap_f32
        self.total = total_f32
        self.off = 0
        self.marks = []

    def f32(self, n):
        assert self.off + n <= self.total, ("SBUF arena overflow", self.off, n, self.total)
        v = self.ap[:, self.off:self.off + n]
        self.off += n
        return v

    def bf16(self, n):
        assert n % 2 == 0
        return self.f32(n // 2).bitcast(BF16)

    def mark(self):
        self.marks.append(self.off)

    def release(self):
        self.off = self.marks.pop()


class Ring:
    def __init__(self, T, name, views):
        self.views = views
        self.bufs = T.bufs_n(name, len(views))
        self.i = 0

    def next(self):
        k = self.i % len(self.views)
        self.i += 1
        return self.views[k], self.bufs[k]


class Prog:
    def __init__(self, mode):
        self.mode = mode
        self.nc = bass.Bass("TRN2", target_bir_lowering=False, num_devices=8)
        self.dr = {}

    def din(self, name, shape, dt=F32):
        self.dr[name] = self.nc.dram_tensor(name, list(shape), dt, kind="ExternalInput").ap()
        return self.dr[name]

    def dout(self, name, shape, dt=F32):
        self.dr[name] = self.nc.dram_tensor(name, list(shape), dt, kind="ExternalOutput").ap()
        return self.dr[name]

    def setup(self, es):
        nc = self.nc
        self.T = Tracker(nc, es)
        ARENA_F32 = 53200
        arena_t = es.enter_context(nc.sbuf_tensor("arena", [128, ARENA_F32], F32))
        self.A = Arena(arena_t[:], ARENA_F32)
        ps_t = es.enter_context(nc.psum_tensor("ps", [128, 8 * 512], F32))
        self.ps = ps_t[:]
        self.psb = self.T.bufs_n("psb", 8)
        A, T = self.A, self.T
        self.X = A.f32(NCH * D).rearrange("p (m d) -> p m d", d=D)
        self.Xb = T.bufs_n("X", NCH)
        self.ident = A.bf16(128)
        self.identb = T.buf("ident")
        self.ones_bf = A.bf16(128)
        self.onesb = T.buf("ones")
        self.gring_v = [A.f32(D) for _ in range(2)]
        self.xs_v = [A.bf16(D) for _ in range(2)]
        self.junk = A.bf16(D)
        self.stat = A.f32(64)
        tmpf = A.f32(128)
        tmpb = T.buf("tmpf")
        self.persist = [self.identb, self.onesb] + self.Xb + self.psb
        T.op("pool", lambda e: e.memset(tmpf, 1.0), writes=[tmpb])
        T.op("pool", lambda e: e.affine_select(out=tmpf, in_=tmpf, pattern=[[-1, 128]], compare_op=ALU.is_equal,
                                               fill=0.0, base=0, channel_multiplier=1), reads=[tmpb], writes=[tmpb])
        T.op("dve", lambda e: e.tensor_copy(out=self.ident, in_=tmpf), reads=[tmpb], writes=[self.identb])
        T.op("pool", lambda e: e.memset(self.ones_bf, 1.0), writes=[self.onesb])
        self.phase_barrier()

    def phase_barrier(self):
        T = self.T
        T.barrier()
        T.bufs = []
        self.gring = Ring(T, "gring", self.gring_v)
        self.xsring = Ring(T, "xs", self.xs_v)
        self.junkb = T.buf("junk")
        self.statb = T.buf("stat")
        for b in self.persist:
            b.w = None
            b.r = {}

    def bank(self, i, n=1):
        return self.ps[:, i * 512:(i + n) * 512]

    def load_x(self, x_dram):
        T = self.T
        xv = x_dram.rearrange("(m p) d -> p m d", p=128)
        for q in range(4):
            T.dma("sp", self.X[:, q * 4:(q + 1) * 4, :], xv[:, q * 4:(q + 1) * 4, :],
                  writes=self.Xb[q * 4:(q + 1) * 4])

    def rms_stats(self, chunks, col0, scr0=32):
        T = self.T
        n = len(chunks)
        ss = self.stat[:, scr0:scr0 + n]
        T.op("dve", lambda e: e.memset(self.stat[:, scr0:scr0 + 2 * n], 0.0), writes=[self.statb])
        for i, m in enumerate(chunks):
            T.op("act", lambda e, i=i, m=m: e.activation(out=self.junk, in_=self.X[:, m, :], func=AF.Square,
                                                         accum_out=self.stat[:, scr0 + i:scr0 + i + 1]),
                 reads=[self.Xb[m]], writes=[self.junkb, self.statb])
        ms = self.stat[:, scr0 + n:scr0 + 2 * n]
        T.op("dve", lambda e: e.tensor_scalar(out=ms, in0=ss, scalar1=1.0 / D, scalar2=NORM_EPS,
                                              op0=ALU.mult, op1=ALU.add), reads=[self.statb], writes=[self.statb])
        T.op("act", lambda e: e.activation(out=ms, in_=ms, func=AF.Ln), reads=[self.statb], writes=[self.statb])
        T.op("act", lambda e: e.activation(out=self.stat[:, col0:col0 + n], in_=ms, func=AF.Exp, scale=-0.5),
             reads=[self.statb], writes=[self.statb])

    def load_gain(self, row):
        T = self.T
        gv, gb = self.gring.next()
        T.dma("sp", gv, self.dr["gains"][row:row + 1, :].broadcast_to([128, D]), writes=[gb])
        return gv, gb

    def norm_to_ht(self, chunks, gain_row, HT, HTb, tok0, trbanks, gain=None, col0=0, scr0=32):
        T = self.T
        gv, gb = gain if gain is not None else self.load_gain(gain_row)
        self.rms_stats(chunks, col0, scr0)
        for i, m in enumerate(chunks):
            xs, xsb = self.xsring.next()
            T.op("dve", lambda e, i=i, m=m, xs=xs: e.scalar_tensor_tensor(
                out=xs, in0=self.X[:, m, :], scalar=self.stat[:, col0 + i:col0 + i + 1], in1=gv,
                op0=ALU.mult, op1=ALU.mult),
                reads=[self.Xb[m], self.statb, gb], writes=[xsb])
            bk = trbanks[i % len(trbanks)]
            trv = self.bank(bk).bitcast(BF16)
            for kc in range(8):
                T.op("pe", lambda e, kc=kc, xs=xs, trv=trv: e.transpose(
                    trv[:, kc * 128:(kc + 1) * 128], xs[:, kc * 128:(kc + 1) * 128], self.ident),
                    reads=[xsb, self.identb], writes=[self.psb[bk]])
            dst = HT[:, :, tok0 + i * 128: tok0 + (i + 1) * 128]
            src = trv.rearrange("p (k t) -> p k t", t=128)
            if i % 2 == 0:
                T.op("act", lambda e, dst=dst, src=src: e.copy(out=dst, in_=src), reads=[self.psb[bk]], writes=[HTb[i]])
            else:
                T.op("dve", lambda e, dst=dst, src=src: e.tensor_copy(out=dst, in_=src), reads=[self.psb[bk]], writes=[HTb[i]])

    def norm_part_a(self, chunks, gain, col0, scr0, ring):
        T = self.T
        gv, gb = gain
        self.rms_stats(chunks, col0, scr0)
        out = []
        for i, m in enumerate(chunks):
            xs, xsb = ring.next()
            T.op("dve", lambda e, i=i, m=m, xs=xs: e.scalar_tensor_tensor(
                out=xs, in0=self.X[:, m, :], scalar=self.stat[:, col0 + i:col0 + i + 1], in1=gv,
                op0=ALU.mult, op1=ALU.mult),
                reads=[self.Xb[m], self.statb, gb], writes=[xsb])
            out.append((xs, xsb))
        return out

    def norm_part_b(self, xss, HT, HTb, tok0, trbanks):
        T = self.T
        for i, (xs, xsb) in enumerate(xss):
            bk = trbanks[i % len(trbanks)]
            trv = self.bank(bk).bitcast(BF16)
            for kc in range(8):
                T.op("pe", lambda e, kc=kc, xs=xs, trv=trv: e.transpose(
                    trv[:, kc * 128:(kc + 1) * 128], xs[:, kc * 128:(kc + 1) * 128], self.ident),
                    reads=[xsb, self.identb], writes=[self.psb[bk]])
            dst = HT[:, :, tok0 + i * 128: tok0 + (i + 1) * 128]
            src = trv.rearrange("p (k t) -> p k t", t=128)
            if i % 2 == 0:
                T.op("act", lambda e, dst=dst, src=src: e.copy(out=dst, in_=src), reads=[self.psb[bk]], writes=[HTb[i]])
            else:
                T.op("dve", lambda e, dst=dst, src=src: e.tensor_copy(out=dst, in_=src), reads=[self.psb[bk]], writes=[HTb[i]])

    def ffn(self, f_idx, gain_row, final=None, gain=None):
        T, A = self.T, self.A
        A.mark()
        G = 6
        HT = A.bf16(8 * TOK).rearrange("p (k t) -> p k t", t=TOK)
        HTb = T.bufs_n("HT", NCH)
        ACTT = A.bf16(G * TOK).rearrange("p (g t) -> p g t", t=TOK)
        ACTb = [[T.buf("actt%d_%d" % (g, ts)) for ts in range(4)] for g in range(G)]
        WOUT = A.bf16(NFC * D).rearrange("p (i n) -> p i n", n=D)
        WOUTb = T.buf("wout")
        winr = Ring(T, "win", [A.bf16(2048) for _ in range(3)])
        sgr = Ring(T, "sg", [A.f32(512) for _ in range(2)])
        win_d = self.dr["ffn_win"]
        wout_d = self.dr["ffn_wout"]
        wq_bounds = [0, 6, 12, 17, 22]
        wout_loaded = [0]

        def load_wout_piece():
            q = wout_loaded[0]
            if q >= 4:
                return
            wout_loaded[0] += 1
            a, b = wq_bounds[q], wq_bounds[q + 1]
            T.dma("pool", WOUT[:, a:b, :], wout_d[f_idx, :, a * D:b * D].rearrange("p (i n) -> p i n", n=D),
                  writes=[WOUTb])
        loaded = []

        def ensure(i):
            while len(loaded) <= min(i + 2, NFC - 1):
                v, b = winr.next()
                T.dma("pool", v, win_d[f_idx, len(loaded), :, :], writes=[b])
                loaded.append((v, b))
        ensure(0)
        gain = gain if gain is not None else self.load_gain(gain_row)
        normed = set()

        xs4 = Ring(T, "xs4", self.xs_v + [A.bf16(D) for _ in range(2)])
        parts = {}

        def part_a(ts):
            parts[ts] = self.norm_part_a([4 * ts + c for c in range(4)], gain, 4 * ts, 32 + 8 * ts, xs4)

        def part_b(ts):
            normed.add(ts)
            self.norm_part_b(parts[ts], HT, HTb[4 * ts:4 * ts + 4], ts * 512, [6, 7])

        def ensure_ht(ts):
            assert ts in normed
        part_a(0)
        part_b(0)
        part_a(1)
        fin_gain = self.load_gain(final[0]) if final is not None else None
        fin_buf = T.buf("finstore") if final is not None else None
        unit = 0
        groups = [list(range(g0, min(g0 + G, NFC))) for g0 in range(0, NFC, G)]
        ob = 0
        ucount = [0]

        def unit_fn(il, i, ts):
            wv, wb = loaded[i]
            ensure_ht(ts)
            bg, bu = (0, 1) if ucount[0] % 2 == 0 else (2, 3)
            ucount[0] += 1
            rhs_b = HTb[ts * 4:(ts + 1) * 4]
            for which, bk in ((0, bg), (1, bu)):
                for kc in range(8):
                    T.op("pe", lambda e, kc=kc, which=which, bk=bk: e.matmul(
                        self.bank(bk), lhsT=wv[:, kc * 256 + which * 128: kc * 256 + which * 128 + 128],
                        rhs=HT[:, kc, ts * 512:(ts + 1) * 512], start=(kc == 0), stop=(kc == 7)),
                        reads=[wb] + rhs_b, writes=[self.psb[bk]])
            sg, sgb = sgr.next()
            T.op("act", lambda e: e.activation(out=sg, in_=self.bank(bg), func=AF.Silu),
                 reads=[self.psb[bg]], writes=[sgb])
            T.op("dve", lambda e: e.tensor_tensor(
                out=ACTT[:, il, ts * 512:(ts + 1) * 512], in0=sg, in1=self.bank(bu), op=ALU.mult),
                reads=[sgb, self.psb[bu]], writes=[ACTb[il][ts]])

        for gi, grp in enumerate(groups):
            if gi == 0:
                ensure(0)
                for _ in range(3):
                    load_wout_piece()
                for ts in range(4):
                    for il in range(3):
                        unit_fn(il, grp[il], ts)
                    if ts + 1 < 4:
                        part_b(ts + 1)
                    if ts + 2 < 4:
                        part_a(ts + 2)
                rest = list(enumerate(grp))[3:]
            else:
                rest = list(enumerate(grp))
            for il, i in rest:
                ensure(i)
                load_wout_piece()
                for ts in range(4):
                    unit_fn(il, i, ts)
            for m in range(NCH):
                b0 = 4 if ob % 2 == 0 else 6
                ob += 1
                for n in range(2):
                    for il, i in enumerate(grp):
                        T.op("pe", lambda e, il=il, i=i, n=n, m=m, b0=b0: e.matmul(
                            self.bank(b0 + n), lhsT=ACTT[:, il, m * 128:(m + 1) * 128],
                            rhs=WOUT[:, i, n * 512:(n + 1) * 512], start=(il == 0), stop=(il == len(grp) - 1)),
                            reads=[ACTb[il][m // 4], WOUTb], writes=[self.psb[b0 + n]])
                T.op("dve", lambda e, m=m, b0=b0: e.scalar_tensor_tensor(
                    out=self.X[:, m, :], in0=self.bank(b0, 2), scalar=0.5, in1=self.X[:, m, :],
                    op0=ALU.mult, op1=ALU.add),
                    reads=[self.psb[b0], self.psb[b0 + 1], self.Xb[m]], writes=[self.Xb[m]])
                if final is not None and grp is groups[-1] and m % 4 == 3:
                    q = m // 4
                    cks = [4 * q + c for c in range(4)]
                    self.rms_stats(cks, 16 + 4 * q, 32 + 8 * q)
                    gv, gb = fin_gain
                    for mm in cks:
                        T.op("dve", lambda e, mm=mm, q=q: e.scalar_tensor_tensor(
                            out=self.X[:, mm, :], in0=self.X[:, mm, :],
                            scalar=self.stat[:, 16 + mm:17 + mm], in1=gv, op0=ALU.mult, op1=ALU.mult),
                            reads=[self.Xb[mm], self.statb, gb], writes=[self.Xb[mm]])
                    ov = final[1].rearrange("(m p) d -> p m d", p=128)
                    T.dma("sp", ov[:, q * 4:(q + 1) * 4, :], self.X[:, q * 4:(q + 1) * 4, :],
                          reads=self.Xb[q * 4:(q + 1) * 4], sembuf=fin_buf)
        self.phase_barrier()
        A.release()

    def store_x(self, out_dram, gain_row=None):
        T, A = self.T, self.A
        A.mark()
        ov = out_dram.rearrange("(m p) d -> p m d", p=128)
        stb = T.buf("store")
        if gain_row is None:
            for q in range(4):
                T.dma("sp", ov[:, q * 4:(q + 1) * 4, :], self.X[:, q * 4:(q + 1) * 4, :],
                      reads=self.Xb[q * 4:(q + 1) * 4], sembuf=stb)
        else:
            gv, gb = self.load_gain(gain_row)
            self.rms_stats(list(range(NCH)), 0)
            for m in range(NCH):
                T.op("dve", lambda e, m=m: e.scalar_tensor_tensor(
                    out=self.X[:, m, :], in0=self.X[:, m, :], scalar=self.stat[:, m:m + 1], in1=gv,
                    op0=ALU.mult, op1=ALU.mult), reads=[self.Xb[m], self.statb, gb], writes=[self.Xb[m]])
                if m % 4 == 3:
                    q = m // 4
                    T.dma("sp", ov[:, q * 4:(q + 1) * 4, :], self.X[:, q * 4:(q + 1) * 4, :],
                          reads=self.Xb[q * 4:(q + 1) * 4], sembuf=stb)
        self.phase_barrier()
        A.release()

    def sgu(self):
        T, A, d = self.T, self.A, self.dr
        A.mark()
        AX = mybir.AxisListType
        GELU = AF.Gelu_apprx_tanh
        wsT_f = A.f32(1024)
        wsTb = T.buf("wsT")
        T.dma("sp", wsT_f, d["a_wsT"][:, :], writes=[wsTb])
        w3 = wsT_f.rearrange("p (g t) -> p g t", t=128)
        T.op("pool", lambda e: e.affine_select(out=w3, in_=w3, pattern=[[0, 8], [1, 128]], compare_op=ALU.is_ge,
                                               fill=0.0, base=0, channel_multiplier=-1), reads=[wsTb], writes=[wsTb])
        wsT_bf = A.bf16(1024)
        wsbfb = T.buf("wsbf")
        T.op("dve", lambda e: e.tensor_copy(out=wsT_bf, in_=wsT_f), reads=[wsTb], writes=[wsbfb])
        BS = A.f32(1024)
        BSb = T.buf("bs")
        T.dma("sp", BS, d["a_bs"][0:1, :].broadcast_to([128, 1024]), writes=[BSb])
        lng = A.f32(16)
        lnb = A.f32(16)
        lnbuf = T.buf("ln")
        T.dma("sp", lng, d["a_lng"][:, :], writes=[lnbuf])
        T.dma("sp", lnb, d["a_lnb"][:, :], writes=[lnbuf])
        T1 = A.f32(16 * 128).rearrange("p (f t) -> p f t", t=128)
        T1b = T.buf("T1")
        for g in range(8):
            T.op("pe", lambda e, g=g: e.matmul(self.ps[:, g * 128:(g + 1) * 128], lhsT=self.ones_bf,
                                                rhs=wsT_bf[:, g * 128:(g + 1) * 128], start=True, stop=True),
                 reads=[self.onesb, wsbfb], writes=[self.psb[g // 4]])
        for fc in range(16):
            g = fc // 2
            T.op("dve", lambda e, fc=fc, g=g: e.scalar_tensor_tensor(
                out=T1[:, fc, :], in0=self.ps[:, g * 128:(g + 1) * 128], scalar=lnb[:, fc:fc + 1],
                in1=BS[:, g * 128:(g + 1) * 128], op0=ALU.mult, op1=ALU.add),
                reads=[self.psb[g // 4], lnbuf, BSb], writes=[T1b])
        HTt = A.bf16(8 * 512).rearrange("p (k t) -> p k t", t=512)
        HTtb = T.bufs_n("HTt", 4)
        VRAW = A.bf16(4 * 2048).rearrange("p (c f) -> p c f", f=2048)
        VRAWb = T.bufs_n("vraw", 4)
        vtr = Ring(T, "vt", [A.f32(256) for _ in range(3)])
        UVT = A.bf16(16 * 512).rearrange("p (f t) -> p f t", t=512)
        UVTb = T.bufs_n("uvt", 16)
        awvr = Ring(T, "awv", [A.bf16(2048) for _ in range(3)])
        awur = Ring(T, "awu", [A.bf16(1024) for _ in range(3)])
        awor = Ring(T, "awo", [A.bf16(1024) for _ in range(8)])
        usbr = Ring(T, "usb", [A.f32(512) for _ in range(2)])
        tmpr = Ring(T, "tmp", [A.f32(512) for _ in range(2)])
        wsTp = [A.bf16(1024) for _ in range(4)]
        wsTpb = T.bufs_n("wsTp", 4)
        NEGMU = [A.bf16(128) for _ in range(4)]
        NEGMUb = T.bufs_n("negmu", 4)
        st = A.f32(128)
        stb = T.buf("sgst")
        SV = st[:, 0:32].rearrange("p (c f) -> p c f", f=8)
        SQ = st[:, 32:64].rearrange("p (c f) -> p c f", f=8)
        S1, S2, MEAN, MSQ, VAR, RSTD, NMU = (st[:, 64 + 4 * i:68 + 4 * i] for i in range(7))
        vb = 0
        gain1 = self.load_gain(1)
        xs4 = Ring(T, "xs4", self.xs_v + [A.bf16(D) for _ in range(2)])

        def sgu_a(tt):
            return self.norm_part_a([4 * tt + c for c in range(4)], gain1, 4 * tt, 32 + 8 * tt, xs4)
        self.norm_part_b(sgu_a(0), HTt, HTtb, 0, [6, 7])
        for tt in range(4):
            chunks = [4 * tt + c for c in range(4)]
            T.op("dve", lambda e: e.memset(st[:, 0:64], 0.0), writes=[stb])
            for ft in range(8):
                av, avb = awvr.next()
                T.dma("pool", av, d["a_win_v"][ft, :, :], writes=[avb])
                for c in range(4):
                    bk = vb % 4
                    vb += 1
                    pv = self.ps[:, bk * 512: bk * 512 + 256]
                    for kc in range(8):
                        T.op("pe", lambda e, kc=kc, c=c, av=av, pv=pv: e.matmul(
                            pv, lhsT=HTt[:, kc, c * 128:(c + 1) * 128], rhs=av[:, kc * 256:(kc + 1) * 256],
                            start=(kc == 0), stop=(kc == 7)), reads=[HTtb[c], avb], writes=[self.psb[bk]])
                    vt, vtb = vtr.next()
                    T.op("act", lambda e, vt=vt, pv=pv, c=c, ft=ft: e.activation(
                        out=vt, in_=pv, func=GELU, accum_out=SV[:, c, ft:ft + 1]),
                        reads=[self.psb[bk]], writes=[vtb, stb])
                    T.op("act", lambda e, vt=vt, c=c, ft=ft: e.activation(
                        out=self.junk[:, 0:256], in_=vt, func=AF.Square, accum_out=SQ[:, c, ft:ft + 1]),
                        reads=[vtb], writes=[self.junkb, stb])
                    T.op("dve", lambda e, vt=vt, c=c, ft=ft: e.tensor_copy(
                        out=VRAW[:, c, ft * 256:(ft + 1) * 256], in_=vt), reads=[vtb], writes=[VRAWb[c]])
            T.op("dve", lambda e: e.tensor_reduce(out=S1, in_=SV, axis=AX.X, op=ALU.add), reads=[stb], writes=[stb])
            T.op("dve", lambda e: e.tensor_reduce(out=S2, in_=SQ, axis=AX.X, op=ALU.add), reads=[stb], writes=[stb])
            T.op("dve", lambda e: e.tensor_scalar(out=MEAN, in0=S1, scalar1=1.0 / 2048, scalar2=None, op0=ALU.mult),
                 reads=[stb], writes=[stb])
            T.op("dve", lambda e: e.tensor_tensor(out=MSQ, in0=MEAN, in1=MEAN, op=ALU.mult), reads=[stb], writes=[stb])
            T.op("dve", lambda e: e.scalar_tensor_tensor(out=VAR, in0=S2, scalar=1.0 / 2048, in1=MSQ,
                                                         op0=ALU.mult, op1=ALU.subtract), reads=[stb], writes=[stb])
            T.op("dve", lambda e: e.tensor_scalar(out=VAR, in0=VAR, scalar1=LN_EPS, scalar2=None, op0=ALU.add),
                 reads=[stb], writes=[stb])
            T.op("act", lambda e: e.activation(out=VAR, in_=VAR, func=AF.Ln), reads=[stb], writes=[stb])
            T.op("act", lambda e: e.activation(out=RSTD, in_=VAR, func=AF.Exp, scale=-0.5), reads=[stb], writes=[stb])
            T.op("dve", lambda e: e.tensor_scalar(out=NMU, in0=MEAN, scalar1=-1.0, scalar2=None, op0=ALU.mult),
                 reads=[stb], writes=[stb])
            for c in range(4):
                T.op("dve", lambda e, c=c: e.tensor_scalar(out=wsTp[c], in0=wsT_f, scalar1=RSTD[:, c:c + 1],
                                                           scalar2=None, op0=ALU.mult),
                     reads=[wsTb, stb], writes=[wsTpb[c]])
                T.op("dve", lambda e, c=c: e.tensor_scalar(out=NEGMU[c], in0=self.ones_bf, scalar1=NMU[:, c:c + 1],
                                                           scalar2=None, op0=ALU.mult),
                     reads=[self.onesb, stb], writes=[NEGMUb[c]])
            aw_first = []
            for fcl in range(8):
                v_, b_ = awor.next()
                T.dma("pool", v_, d["a_wout"][:, fcl * D:(fcl + 1) * D], writes=[b_])
                aw_first.append((v_, b_))
            for fc in range(16):
                g = fc // 2
                au, aub = awur.next()
                T.dma("pool", au, d["a_win_u"][fc, :, :], writes=[aub])
                bu = 4 + fc % 2
                bs_ = 6 + fc % 2
                for kc in range(8):
                    T.op("pe", lambda e, kc=kc, au=au, bu=bu: e.matmul(
                        self.bank(bu), lhsT=au[:, kc * 128:(kc + 1) * 128], rhs=HTt[:, kc, :],
                        start=(kc == 0), stop=(kc == 7)), reads=[aub] + HTtb, writes=[self.psb[bu]])
                usb, usbb = usbr.next()
                T.op("act", lambda e, usb=usb, bu=bu: e.activation(out=usb, in_=self.bank(bu), func=GELU),
                     reads=[self.psb[bu]], writes=[usbb])
                for c in range(4):
                    po = self.ps[:, bs_ * 512 + c * 128: bs_ * 512 + (c + 1) * 128]
                    T.op("pe", lambda e, c=c, fc=fc, g=g, po=po: e.matmul(
                        po, lhsT=VRAW[:, c, fc * 128:(fc + 1) * 128], rhs=wsTp[c][:, g * 128:(g + 1) * 128],
                        start=True, stop=False), reads=[VRAWb[c], wsTpb[c]], writes=[self.psb[bs_]])
                    T.op("pe", lambda e, c=c, g=g, po=po: e.matmul(
                        po, lhsT=NEGMU[c], rhs=wsTp[c][:, g * 128:(g + 1) * 128], start=False, stop=True),
                        reads=[NEGMUb[c], wsTpb[c]], writes=[self.psb[bs_]])
                tmp, tmpb = tmpr.next()
                T.op("dve", lambda e, tmp=tmp, fc=fc, bs_=bs_: e.scalar_tensor_tensor(
                    out=tmp.rearrange("p (c t) -> p c t", t=128),
                    in0=self.bank(bs_).rearrange("p (c t) -> p c t", t=128), scalar=lng[:, fc:fc + 1],
                    in1=T1[:, fc:fc + 1, :].broadcast_to([128, 4, 128]), op0=ALU.mult, op1=ALU.add),
                    reads=[self.psb[bs_], lnbuf, T1b], writes=[tmpb])
                T.op("dve", lambda e, tmp=tmp, usb=usb, fc=fc: e.tensor_tensor(
                    out=UVT[:, fc, :], in0=tmp, in1=usb, op=ALU.mult), reads=[tmpb, usbb], writes=[UVTb[fc]])
            nxt = sgu_a(tt + 1) if tt + 1 < 4 else None
            ob = 0
            for hf in range(2):
                aw = aw_first if hf == 0 else []
                for fcl in range(8 if hf == 1 else 0):
                    v_, b_ = awor.next()
                    fc = hf * 8 + fcl
                    T.dma("pool", v_, d["a_wout"][:, fc * D:(fc + 1) * D], writes=[b_])
                    aw.append((v_, b_))
                for c in range(4):
                    b0 = 0 if ob % 2 == 0 else 2
                    ob += 1
                    for n in range(2):
                        for fcl in range(8):
                            fc = hf * 8 + fcl
                            T.op("pe", lambda e, fc=fc, fcl=fcl, c=c, n=n, b0=b0: e.matmul(
                                self.bank(b0 + n), lhsT=UVT[:, fc, c * 128:(c + 1) * 128],
                                rhs=aw[fcl][0][:, n * 512:(n + 1) * 512], start=(fcl == 0), stop=(fcl == 7)),
                                reads=[UVTb[fc], aw[fcl][1]], writes=[self.psb[b0 + n]])
                    m = chunks[c]
                    T.op("dve", lambda e, m=m, b0=b0: e.tensor_tensor(
                        out=self.X[:, m, :], in0=self.bank(b0, 2), in1=self.X[:, m, :], op=ALU.add),
                        reads=[self.psb[b0], self.psb[b0 + 1], self.Xb[m]], writes=[self.Xb[m]])
            if nxt is not None:
                self.norm_part_b(nxt, HTt, HTtb, 0, [6, 7])
        self.phase_barrier()
        A.release()

    def kv_exchange(self):
        T, A, d = self.T, self.A, self.dr
        nc = self.nc
        A.mark()
        HT = A.bf16(8 * TOK).rearrange("p (k t) -> p k t", t=TOK)
        HTb = T.bufs_n("HT", NCH)
        wk = A.bf16(2048)
        wv = A.bf16(2048)
        wkb, wvb = T.buf("wk"), T.buf("wv")
        T.dma("pool", wk, d["w_kv_k"][:, :], writes=[wkb])
        T.dma("pool", wv, d["w_kv_v"][:, :], writes=[wvb])
        KVS = A.bf16(8192)
        KVSb = T.buf("kvs")
        gain3 = self.load_gain(3)
        u = 0
        xs4 = Ring(T, "xs4", self.xs_v + [A.bf16(D) for _ in range(2)])
        kparts = {}

        def kv_a(ts):
            kparts[ts] = self.norm_part_a([4 * ts + c for c in range(4)], gain3, 4 * ts, 32 + 8 * ts, xs4)

        def kv_b(ts):
            self.norm_part_b(kparts[ts], HT, HTb[4 * ts:4 * ts + 4], ts * 512, [6, 7])
        kv_a(0)
        kv_b(0)
        kv_a(1)
        for ts in range(4):
            if ts > 0:
                pass
            for jj in range(2):
                bk = u % 4
                u += 1
                for kc in range(8):
                    T.op("pe", lambda e, kc=kc, jj=jj, ts=ts, bk=bk: e.matmul(
                        self.bank(bk), lhsT=wk[:, kc * 256 + jj * 128: kc * 256 + jj * 128 + 128],
                        rhs=HT[:, kc, ts * 512:(ts + 1) * 512], start=(kc == 0), stop=(kc == 7)),
                        reads=[wkb] + HTb[ts * 4:(ts + 1) * 4], writes=[self.psb[bk]])
                dst = KVS[:, jj * TOK + ts * 512: jj * TOK + (ts + 1) * 512]
                if u % 2 == 0:
                    T.op("dve", lambda e, dst=dst, bk=bk: e.tensor_copy(out=dst, in_=self.bank(bk)),
                         reads=[self.psb[bk]], writes=[KVSb])
                else:
                    T.op("act", lambda e, dst=dst, bk=bk: e.copy(out=dst, in_=self.bank(bk)),
                         reads=[self.psb[bk]], writes=[KVSb])
            for m in range(4 * ts, 4 * ts + 4):
                bk = u % 4
                u += 1
                pv = self.ps[:, bk * 512: bk * 512 + 256]
                for kc in range(8):
                    T.op("pe", lambda e, kc=kc, m=m, pv=pv: e.matmul(
                        pv, lhsT=HT[:, kc, m * 128:(m + 1) * 128], rhs=wv[:, kc * 256:(kc + 1) * 256],
                        start=(kc == 0), stop=(kc == 7)), reads=[wvb, HTb[m]], writes=[self.psb[bk]])
                dst = KVS[:, 4096 + m * 256: 4096 + (m + 1) * 256]
                if u % 2 == 0:
                    T.op("dve", lambda e, dst=dst, pv=pv: e.tensor_copy(out=dst, in_=pv), reads=[self.psb[bk]], writes=[KVSb])
                else:
                    T.op("act", lambda e, dst=dst, pv=pv: e.copy(out=dst, in_=pv), reads=[self.psb[bk]], writes=[KVSb])
            if ts + 1 < 4:
                kv_b(ts + 1)
            if ts + 2 < 4:
                kv_a(ts + 2)
        bounce = nc.dram_tensor("kv_bounce", [128, 4096], F32)
        self.gathered = nc.dram_tensor("kv_gathered", [256, 4096], F32)
        bb = T.buf("bounce")
        gb = T.buf("gathered")
        T.dma("sp", bounce[:, :], KVS.bitcast(F32), reads=[KVSb], writes=[bb])
        T.dma("pool", None, None, reads=[bb], writes=[gb], inc=1,
              fn=lambda e: e.collective_compute("AllGather", ALU.bypass,
                                                replica_groups=[[0, 1], [2, 3], [4, 5], [6, 7]],
                                                ins=[bounce.ap().opt()], outs=[self.gathered.ap().opt()]))
        self.phase_barrier()
        A.release()

    def attention(self):
        T, A, d = self.T, self.A, self.dr
        A.mark()
        gain5 = self.load_gain(5)
        KVr = [A.bf16(8192) for _ in range(2)]
        KTb = T.buf("KT")
        Vb = KTb
        gat = self.gathered.ap()
        WQ = A.bf16(8 * D).rearrange("p (k n) -> p k n", n=D)
        WO = A.bf16(8 * D).rearrange("p (k n) -> p k n", n=D)
        WQb, WOb = T.buf("WQ"), T.buf("WO")
        T.dma("pool", WQ, d["w_q"][:, :].rearrange("p (k n) -> p k n", n=D), writes=[WQb])
        for h in range(2):
            T.dma("sp", KVr[h].bitcast(F32), gat[h * 128:(h + 1) * 128, :], writes=[KTb])

        def KTblk(r0, jj, kb):
            h, jl_ = chunk_owner(kb)
            return KVr[h][r0:r0 + 64, jj * TOK + jl_ * 128: jj * TOK + (jl_ + 1) * 128]

        def Vblk(kb, k):
            h, jl_ = chunk_owner(kb)
            return KVr[h][:, 4096 + jl_ * 256 + k * 64: 4096 + jl_ * 256 + (k + 1) * 64]
        MASK = A.bf16(4 * 512)
        MASKb = T.buf("mask")
        T.dma("pool", MASK, d["masks"][:, :], writes=[MASKb])
        T.dma("pool", WO, d["w_o"][:, :].rearrange("p (k n) -> p k n", n=D), writes=[WOb])
        tf = self.junk.bitcast(F32)[:, 0:128]
        tfb = self.junkb
        TRI = A.bf16(128)
        COTRI = A.bf16(128)
        TRIb, COTRIb = T.buf("tri"), T.buf("cotri")
        T.op("pool", lambda e: e.memset(tf, 1.0), writes=[tfb])
        T.op("pool", lambda e: e.affine_select(out=tf, in_=tf, pattern=[[-1, 128]], compare_op=ALU.is_ge, fill=0.0,
                                               base=0, channel_multiplier=1), reads=[tfb], writes=[tfb])
        T.op("dve", lambda e: e.tensor_copy(out=TRI, in_=tf), reads=[tfb], writes=[TRIb])
        T.op("pool", lambda e: e.memset(tf, 1.0), reads=[tfb], writes=[tfb])
        T.op("pool", lambda e: e.affine_select(out=tf, in_=tf, pattern=[[1, 128]], compare_op=ALU.is_gt, fill=0.0,
                                               base=0, channel_multiplier=-1), reads=[tfb], writes=[tfb])
        T.op("dve", lambda e: e.tensor_copy(out=COTRI, in_=tf), reads=[tfb], writes=[COTRIb])
        HTq = A.bf16(8 * 512).rearrange("p (k t) -> p k t", t=512)
        HTqb = T.bufs_n("HTq", 4)
        QT2 = [A.bf16(8 * 512).rearrange("p (h t) -> p h t", t=512) for _ in range(2)]
        QTb2 = [T.bufs_n("QT%d_" % i, 4) for i in range(2)]
        otr = Ring(T, "OT", [A.bf16(8 * 128).rearrange("p (h t) -> p h t", t=128) for _ in range(2)])
        Er = Ring(T, "E", [A.f32(1024) for _ in range(3)])
        SPr = Ring(T, "SP", [A.bf16(1024) for _ in range(4)])
        Wr = Ring(T, "W", [A.f32(1024) for _ in range(2)])
        Ar = Ring(T, "A", [A.bf16(1024) for _ in range(2)])
        zpairs = [(0, 1)]
        PB = 4
        OB = 6
        zi = [0]

        def stage_a(u):
            pp, kb, jl, j, kmax = u["pp"], u["kb"], u["jl"], u["j"], u["kmax"]
            QT, QTb = u["QT"]
            zb = zpairs[zi[0] % len(zpairs)]
            zi[0] += 1
            for c in range(2):
                k = 2 * pp + c
                r0 = c * 64
                T.op("pe", lambda e, c=c, r0=r0: e.matmul(
                    self.bank(zb[c]), lhsT=KTblk(r0, pp, kb),
                    rhs=QT[r0:r0 + 64, pp * 4:pp * 4 + 4, jl * 128:(jl + 1) * 128], start=True, stop=True),
                    reads=[KTb, QTb[k]], writes=[self.psb[zb[c]]])
            zbufs = [self.psb[zb[0]], self.psb[zb[1]]]
            E, Eb = Er.next()
            T.op("act", lambda e: e.activation(out=E, in_=self.bank(zb[0], 2), func=AF.Exp), reads=zbufs, writes=[Eb])
            u.update(E=E, Eb=Eb)

        def stage_a2(u):
            kb, j, kmax = u["kb"], u["j"], u["kmax"]
            E, Eb = u["E"], u["Eb"]
            if kb >= kmax - 1:
                mi = (j % 2) * 2 + (0 if kb == kmax else 1)
                mk = MASK[:, mi * 512:(mi + 1) * 512]
                for c in range(2):
                    T.op("dve", lambda e, c=c: e.tensor_tensor(out=E[:, c * 512:(c + 1) * 512],
                                                               in0=E[:, c * 512:(c + 1) * 512], in1=mk, op=ALU.mult),
                         reads=[Eb, MASKb], writes=[Eb])
            SP, SPb = SPr.next()
            T.op("act", lambda e: e.activation(out=SP, in_=E, func=AF.Ln, bias=1.0), reads=[Eb], writes=[SPb])
            u.update(SP=SP, SPb=SPb)

        def stage_b(u):
            ch = u["chain"]
            pbufs = [self.psb[PB], self.psb[PB + 1]]
            for c in range(2):
                sl = slice(c * 512, (c + 1) * 512)
                if u["first"]:
                    T.op("pe", lambda e, c=c, sl=sl: e.matmul(self.bank(PB + c), lhsT=TRI, rhs=u["SP"][:, sl],
                                                              start=True, stop=True),
                         reads=[TRIb, u["SPb"]], writes=[pbufs[c]])
                else:
                    pv = ch["prev"]
                    T.op("pe", lambda e, c=c, sl=sl, pv=pv: e.matmul(
                        self.bank(PB + c), lhsT=COTRI, rhs=pv["SP"][:, sl], start=False, stop=False,
                        skip_group_check=True), reads=[COTRIb, pv["SPb"]], writes=[pbufs[c]])
                    T.op("pe", lambda e, c=c, sl=sl: e.matmul(
                        self.bank(PB + c), lhsT=TRI, rhs=u["SP"][:, sl], start=False, stop=True,
                        skip_group_check=True), reads=[TRIb, u["SPb"]], writes=[pbufs[c]])
            ch["prev"] = u
            W, Wb = Wr.next()
            T.op("act", lambda e: e.activation(out=W, in_=self.bank(PB, 2), func=AF.Exp, scale=-1.0),
                 reads=pbufs, writes=[Wb])
            Av, Ab = Ar.next()
            T.op("dve", lambda e: e.tensor_tensor(out=Av, in0=u["E"], in1=W, op=ALU.mult),
                 reads=[u["Eb"], Wb], writes=[Ab])
            u.update(A=Av, Ab=Ab)

        def stage_c(u):
            pp, kb = u["pp"], u["kb"]
            OB = u["chain"]["ob"]
            for c in range(2):
                k = 2 * pp + c
                a3 = u["A"][:, c * 512:(c + 1) * 512].rearrange("p (h t) -> p h t", t=128)
                for hh in range(2):
                    T.op("pe", lambda e, c=c, k=k, hh=hh, a3=a3: e.matmul(
                        self.ps[hh * 64:(hh + 1) * 64, OB * 512 + c * 256: OB * 512 + (c + 1) * 256],
                        lhsT=Vblk(kb, k), rhs=a3[:, hh::2, :],
                        start=(u["first"] and c == 0), stop=u["last"], skip_group_check=True),
                        reads=[Vb, u["Ab"]], writes=[self.psb[OB]])
            if u["last"]:
                OT, OTb = u["OT"]
                T.op("dve", lambda e: e.tensor_copy(
                    out=OT[:, 4 * pp:4 * pp + 4, :],
                    in_=self.bank(OB).rearrange("p (h t) -> p h t", t=128)),
                    reads=[self.psb[OB]], writes=[OTb])

        def out_proj(j, OTv, n_):
            OT, OTb = OTv
            for hp in range(8):
                T.op("pe", lambda e, hp=hp: e.matmul(
                    self.bank(3), lhsT=OT[:, hp, :], rhs=WO[:, hp, n_ * 512:(n_ + 1) * 512],
                    start=(hp == 0), stop=(hp == 7)), reads=[OTb, WOb], writes=[self.psb[3]])
            T.op("dve", lambda e: e.tensor_tensor(
                out=self.X[:, j, n_ * 512:(n_ + 1) * 512], in0=self.bank(3),
                in1=self.X[:, j, n_ * 512:(n_ + 1) * 512], op=ALU.add),
                reads=[self.psb[3], self.Xb[j]], writes=[self.Xb[j]])

        def qproj_slices(qb):
            chunks = [4 * qb + c for c in range(4)]
            QT, QTb = QT2[qb % 2], QTb2[qb % 2]
            gv, gb = gain5
            col0, scr0 = 4 * qb, 32 + 8 * qb
            sl = [lambda: self.rms_stats(chunks, col0, scr0)]

            def norm_chunk(i, m):
                xs, xsb = self.xsring.next()
                T.op("dve", lambda e: e.scalar_tensor_tensor(
                    out=xs, in0=self.X[:, m, :], scalar=self.stat[:, col0 + i:col0 + i + 1], in1=gv,
                    op0=ALU.mult, op1=ALU.mult), reads=[self.Xb[m], self.statb, gb], writes=[xsb])
                trv = self.bank(7).bitcast(BF16)
                for kc in range(8):
                    T.op("pe", lambda e, kc=kc: e.transpose(
                        trv[:, kc * 128:(kc + 1) * 128], xs[:, kc * 128:(kc + 1) * 128], self.ident),
                        reads=[xsb, self.identb], writes=[self.psb[7]])
                T.op("dve", lambda e: e.tensor_copy(out=HTq[:, :, i * 128:(i + 1) * 128],
                                                    in_=trv.rearrange("p (k t) -> p k t", t=128)),
                     reads=[self.psb[7]], writes=[HTqb[i]])
            for i, m in enumerate(chunks):
                sl.append(lambda i=i, m=m: norm_chunk(i, m))

            def qhead(h):
                k = h // 4
                r0 = (k % 2) * 64
                slot = (k // 2) * 4 + h % 4
                pq = self.ps[r0:r0 + 64, 7 * 512:8 * 512]
                for kc in range(8):
                    T.op("pe", lambda e, kc=kc: e.matmul(
                        pq, lhsT=WQ[:, kc, h * 64:(h + 1) * 64], rhs=HTq[:, kc, :], start=(kc == 0), stop=(kc == 7)),
                        reads=[WQb] + HTqb, writes=[self.psb[7]])
                T.op("dve", lambda e: e.tensor_scalar(
                    out=QT[r0:r0 + 64, slot, :], in0=pq, scalar1=0.125, scalar2=None, op0=ALU.mult),
                    reads=[self.psb[7]], writes=[QTb[k]])
            for h in range(16):
                sl.append(lambda h=h: qhead(h))
            return sl

        units = []
        nchain = [0]
        oproj = []
        batch_first_unit = {}
        for qb in range(4):
            batch_first_unit[qb] = len(units)
            for jl in range(4):
                j = 4 * qb + jl
                kmax = 4 * (j // 2) + (1 if j % 2 == 0 else 3)
                OTv = otr.next()
                for pp in range(2):
                    chain = dict(prev=None, ob=(6 if nchain[0] % 2 == 0 else 2))
                    nchain[0] += 1
                    for kb in range(kmax, -1, -1):
                        units.append(dict(pp=pp, kb=kb, jl=jl, j=j, kmax=kmax, first=(kb == kmax), last=(kb == 0),
                                          chain=chain, OT=OTv, QT=(QT2[qb % 2], QTb2[qb % 2]), pre=[], post=None))
                oproj.append((len(units) - 1, j, OTv))
        for f in qproj_slices(0):
            f()
        for qb in range(3):
            sl = qproj_slices(qb + 1)
            u0 = batch_first_unit[qb]
            nb = batch_first_unit[qb + 1] - u0
            assert nb >= len(sl)
            for i, f in enumerate(sl):
                units[u0 + i]["pre"].append(f)
        n = len(units)
        posts = {}
        for (ui, j, OTv) in oproj:
            for n_ in range(2):
                posts.setdefault(ui + 2 + 2 * n_, []).append((lambda j=j, OTv=OTv, n_=n_: out_proj(j, OTv, n_)))
        for t in range(n + 2):
            if t < n:
                for f in units[t]["pre"]:
                    f()
                stage_a(units[t])
            if 1 <= t <= n:
                stage_b(units[t - 1])
            if t < n:
                stage_a2(units[t])
            if t >= 2:
                stage_c(units[t - 2])
                for f in posts.pop(t - 2, []):
                    f()
        for k_ in sorted(posts):
            for f in posts[k_]:
                f()
        self.phase_barrier()
        A.release()


def build(mode="FUSED", stages=None):
    P = Prog(mode)
    P.din("x", [TOK, D])
    P.din("gains", [8, D])
    P.din("ffn_win", [4, NFC, 128, 2048])
    P.din("ffn_wout", [4, 128, NFC * D])
    P.din("a_wsT", [128, 1024])
    P.din("a_bs", [1, 1024])
    P.din("a_lng", [128, 16])
    P.din("a_lnb", [128, 16])
    P.din("a_win_v", [8, 128, 2048])
    P.din("a_win_u", [16, 128, 1024])
    P.din("a_wout", [128, 16 * D])
    P.din("w_kv_k", [128, 2048])
    P.din("w_kv_v", [128, 2048])
    P.din("w_q", [128, 8 * D])
    P.din("w_o", [128, 8 * D])
    P.din("masks", [128, 4 * 512])
    P.dout("y", [TOK, D])
    es = contextlib.ExitStack()
    with es:
        P.setup(es)
        g0 = P.load_gain(0)
        P.load_x(P.dr["x"])
        st = stages or ["ffn0a", "sgu", "ffn0b", "kv", "ffn1a", "attn", "ffn1b", "final"]
        if "ffn0a" in st:
            P.ffn(0, 0, gain=g0)
        if "sgu" in st:
            P.sgu()
        if "ffn0b" in st:
            P.ffn(1, 2)
        if "kv" in st:
            P.kv_exchange()
        if "ffn1a" in st:
            P.ffn(2, 4)
        if "attn" in st:
            P.attention()
        if "ffn1b" in st and "final" in st:
            P.ffn(3, 6, final=(7, P.dr["y"]))
        else:
            if "ffn1b" in st:
                P.ffn(3, 6)
            P.store_x(P.dr["y"], gain_row=(7 if "final" in st else None))
    return P.nc


def prep_weights(inp):
    f = np.float32
    w = {}
    win = np.asarray(inp["ffn_w_in"], f).reshape(4, 8, 128, 2, NFC, 128)
    w["ffn_win"] = np.ascontiguousarray(win.transpose(0, 4, 2, 1, 3, 5)).reshape(4, NFC, 128, 2048)
    wout = np.asarray(inp["ffn_w_out"], f).reshape(4, NFC, 128, D)
    w["ffn_wout"] = np.ascontiguousarray(wout.transpose(0, 2, 1, 3)).reshape(4, 128, NFC * D)
    g = np.zeros((8, D), f)
    fn = np.asarray(inp["ffn_norm"], f)
    mn = np.asarray(inp["mix_norm"], f)
    g[0], g[1], g[2], g[3] = fn[0, 0], mn[0], fn[0, 1], np.asarray(inp["kv_norm"], f)
    g[4], g[5], g[6], g[7] = fn[1, 0], mn[1], fn[1, 1], np.asarray(inp["final_norm"], f)
    w["gains"] = g
    return w


def prep_l0(inp):
    f = np.float32
    w = {}
    w["a_wsT"] = np.ascontiguousarray(np.asarray(inp["a_w_s"], f)[0].transpose(2, 0, 1)).reshape(128, 1024)
    w["a_bs"] = np.ascontiguousarray(np.asarray(inp["a_b_s"], f)[0]).reshape(1, 1024)
    w["a_lng"] = np.ascontiguousarray(np.asarray(inp["a_ln_g"], f)[0].reshape(16, 128).T)
    w["a_lnb"] = np.ascontiguousarray(np.asarray(inp["a_ln_b"], f)[0].reshape(16, 128).T)
    awin = np.asarray(inp["a_w_in"], f)[0]
    w["a_win_v"] = np.ascontiguousarray(awin[:, 2048:].reshape(8, 128, 8, 256).transpose(2, 1, 0, 3)).reshape(8, 128, 2048)
    w["a_win_u"] = np.ascontiguousarray(awin[:, :2048].reshape(8, 128, 16, 128).transpose(2, 1, 0, 3)).reshape(16, 128, 1024)
    w["a_wout"] = np.ascontiguousarray(np.asarray(inp["a_w_out"], f)[0].reshape(16, 128, D).transpose(1, 0, 2)).reshape(128, 16 * D)
    wkv = np.asarray(inp["w_kv"], f).reshape(8, 128, 512)
    w["w_kv_k"] = np.ascontiguousarray(wkv[:, :, :256].transpose(1, 0, 2)).reshape(128, 2048)
    w["w_kv_v"] = np.ascontiguousarray(wkv[:, :, 256:].transpose(1, 0, 2)).reshape(128, 2048)
    return w


def prep_l1(inp):
    f = np.float32
    w = {}
    w["w_q"] = np.ascontiguousarray(np.asarray(inp["b_w_q"], f)[0].reshape(8, 128, D).transpose(1, 0, 2)).reshape(128, 8 * D)
    w["w_o"] = np.ascontiguousarray(np.asarray(inp["b_w_o"], f)[0].reshape(8, 128, D).transpose(1, 0, 2)).reshape(128, 8 * D)
    return w


def core_masks(h):
    s = np.arange(128)[:, None]
    t = np.arange(128)[None, :]
    diag = np.tile((s < t).astype(np.float32), (1, 4))
    ones = np.ones((128, 512), np.float32)
    zeros = np.zeros((128, 512), np.float32)
    if h == 0:
        m = [zeros, diag, diag, ones]
    else:
        m = [diag, ones, zeros, diag]
    return np.ascontiguousarray(np.concatenate(m, axis=1))


def shard_x(x):
    x = np.asarray(x, np.float32)
    out = []
    for c in range(8):
        b, h = divmod(c, 2)
        rows = np.concatenate([np.arange(local_to_global(h, j) * 128, local_to_global(h, j) * 128 + 128)
                               for j in range(NCH)])
        out.append(np.ascontiguousarray(x[b][rows]))
    return out


def unshard(ys):
    out = np.zeros((4, S, D), np.float32)
    for c in range(8):
        b, h = divmod(c, 2)
        for j in range(NCH):
            g = local_to_global(h, j)
            out[b, g * 128:(g + 1) * 128] = ys[c][j * 128:(j + 1) * 128]
    return out


def kernel(**inp):
    w = dict(prep_weights(inp), **prep_l0(inp))
    w.update(prep_l1(inp))
    xs = shard_x(inp["x"])
    nc = build()
    in_maps = [dict(w, x=xs[c], masks=core_masks(c % 2)) for c in range(8)]
    res = run_bass_kernel_spmd(nc, in_maps, core_ids=list(range(8))).results
    return unshard([r["y"] for r in res])
```

```python
import contextlib
import numpy as np
import concourse.bass as bass
import concourse.mybir as mybir
from concourse.bass_utils import run_bass_kernel_spmd

F32 = mybir.dt.float32
BF16 = mybir.dt.bfloat16
AF = mybir.ActivationFunctionType
ALU = mybir.AluOpType

D = 1024
DFF = 2816
NFC = 22
S = 4096
NCH = 16
TOK = NCH * 128
NORM_EPS = 1e-6
LN_EPS = 1e-5


def chunk_owner(gb):
    m, r = divmod(gb, 4)
    if r == 0:
        return 0, 2 * m
    if r == 3:
        return 0, 2 * m + 1
    if r == 1:
        return 1, 2 * m
    return 1, 2 * m + 1


def local_to_global(h, j):
    m, r = divmod(j, 2)
    if h == 0:
        return 4 * m + (0 if r == 0 else 3)
    return 4 * m + (1 if r == 0 else 2)


class Buf:
    __slots__ = ("name", "w", "r", "dsem")

    def __init__(self, name):
        self.name = name
        self.w = None
        self.r = {}
        self.dsem = None


class EngS:
    def __init__(self, key, eng, sem):
        self.key = key
        self.eng = eng
        self.sem = sem
        self.count = 0
        self.seen = {}


class Tracker:
    def __init__(self, nc, es, n_dma_sems=40):
        self.nc = nc
        self.E = {}
        for key, eng in (("pe", nc.tensor), ("act", nc.scalar), ("dve", nc.vector),
                         ("pool", nc.gpsimd), ("sp", nc.sync)):
            self.E[key] = EngS(key, eng, es.enter_context(nc.semaphore("s_" + key)))
        self.dpool = [["d%d" % i, es.enter_context(nc.semaphore("s_d%d" % i)), 0] for i in range(n_dma_sems)]
        self.dfree = {"pool": list(range(0, 18)), "sp": list(range(18, 38)), "cc": list(range(38, n_dma_sems))}
        self.downer = {}
        self.dused = {}
        self.bufs = []

    def buf(self, name):
        b = Buf(name)
        self.bufs.append(b)
        return b

    def bufs_n(self, name, n):
        return [self.buf("%s%d" % (name, i)) for i in range(n)]

    def _deps(self, reads, writes):
        deps = {}

        def add(tok):
            if tok is None:
                return
            k, s, v = tok
            if k not in deps or deps[k][1] < v:
                deps[k] = (s, v)
        for b in reads:
            add(b.w)
        for b in writes:
            add(b.w)
            for t in b.r.values():
                add(t)
        return deps

    def _wait(self, e, deps):
        for k, (s, v) in deps.items():
            if e.key == "pe" and k == "pe":
                continue
            if e.seen.get(k, 0) >= v:
                continue
            e.eng.wait_ge(s, v)
            e.seen[k] = v

    def _mark(self, tok, reads, writes):
        k = tok[0]
        for b in reads:
            b.r[k] = tok
        for b in writes:
            b.w = tok
            b.r = {}

    def op(self, en, fn, reads=(), writes=()):
        e = self.E[en]
        self._wait(e, self._deps(reads, writes))
        inst = fn(e.eng)
        e.count += 1
        inst.then_inc(e.sem, 1)
        tok = (e.key, e.sem, e.count)
        self._mark(tok, reads, writes)
        return tok

    def dma(self, en, out, in_, reads=(), writes=(), sembuf=None, fn=None, inc=16):
        e = self.E[en]
        self._wait(e, self._deps(reads, writes))
        sb = sembuf if sembuf is not None else (writes[0] if writes else reads[0])
        if sb.dsem is None:
            cls = "cc" if fn is not None else en
            sb.dsem = self.dfree[cls].pop()
            self.downer[sb.dsem] = cls
            self.dused[id(sb)] = sb
        ent = self.dpool[sb.dsem]
        inst = fn(e.eng) if fn is not None else e.eng.dma_start(out=out, in_=in_)
        ent[2] += inc
        inst.then_inc(ent[1], inc)
        tok = (ent[0], ent[1], ent[2])
        self._mark(tok, reads, writes)
        return tok

    def barrier(self):
        deps = {}
        for e in self.E.values():
            if e.count:
                deps[e.key] = (e.sem, e.count)
        for ent in self.dpool:
            if ent[2]:
                deps[ent[0]] = (ent[1], ent[2])
        for e in self.E.values():
            d = {k: v for k, v in deps.items() if k != e.key}
            self._wait(e, d)
        for b in self.bufs:
            b.w = None
            b.r = {}
            if b.dsem is not None:
                self.dfree[self.downer[b.dsem]].append(b.dsem)
                b.dsem = None
        self.dused = {}
        self.bufs = []


class Arena:
    def __init__(self, ap_f32, total_f32):
        self.ap = ap_f32
        self.total = total_f32
        self.off = 0
        self.marks = []

    def f32(self, n):
        assert self.off + n <= self.total, ("SBUF arena overflow", self.off, n, self.total)
        v = self.ap[:, self.off:self.off + n]
        self.off += n
        return v

    def bf16(self, n):
        assert n % 2 == 0
        return self.f32(n // 2).bitcast(BF16)

    def mark(self):
        self.marks.append(self.off)

    def release(self):
        self.off = self.marks.pop()


class Ring:
    def __init__(self, T, name, views):
        self.views = views
        self.bufs = T.bufs_n(name, len(views))
        self.i = 0

    def next(self):
        k = self.i % len(self.views)
        self.i += 1
        return self.views[k], self.bufs[k]


class Prog:
    def __init__(self, mode):
        self.mode = mode
        self.nc = bass.Bass("TRN2", target_bir_lowering=False, num_devices=8)
        self.dr = {}

    def din(self, name, shape, dt=F32):
        self.dr[name] = self.nc.dram_tensor(name, list(shape), dt, kind="ExternalInput").ap()
        return self.dr[name]

    def dout(self, name, shape, dt=F32):
        self.dr[name] = self.nc.dram_tensor(name, list(shape), dt, kind="ExternalOutput").ap()
        return self.dr[name]

    def setup(self, es):
        nc = self.nc
        self.T = Tracker(nc, es)
        ARENA_F32 = 53200
        arena_t = es.enter_context(nc.sbuf_tensor("arena", [128, ARENA_F32], F32))
        self.A = Arena(arena_t[:], ARENA_F32)
        ps_t = es.enter_context(nc.psum_tensor("ps", [128, 8 * 512], F32))
        self.ps = ps_t[:]
        self.psb = self.T.bufs_n("psb", 8)
        A, T = self.A, self.T
        self.X = A.f32(NCH * D).rearrange("p (m d) -> p m d", d=D)
        self.Xb = T.bufs_n("X", NCH)
        self.ident = A.bf16(128)
        self.identb = T.buf("ident")
        self.ones_bf = A.bf16(128)
        self.onesb = T.buf("ones")
        self.gring_v = [A.f32(D) for _ in range(2)]
        self.xs_v = [A.bf16(D) for _ in range(2)]
        self.junk = A.bf16(D)
        self.stat = A.f32(64)
        tmpf = A.f32(128)
        tmpb = T.buf("tmpf")
        self.persist = [self.identb, self.onesb] + self.Xb + self.psb
        T.op("pool", lambda e: e.memset(tmpf, 1.0), writes=[tmpb])
        T.op("pool", lambda e: e.affine_select(out=tmpf, in_=tmpf, pattern=[[-1, 128]], compare_op=ALU.is_equal,
                                               fill=0.0, base=0, channel_multiplier=1), reads=[tmpb], writes=[tmpb])
        T.op("dve", lambda e: e.tensor_copy(out=self.ident, in_=tmpf), reads=[tmpb], writes=[self.identb])
        T.op("pool", lambda e: e.memset(self.ones_bf, 1.0), writes=[self.onesb])
        self.phase_barrier()

    def phase_barrier(self):
        T = self.T
        T.barrier()
        T.bufs = []
        self.gring = Ring(T, "gring", self.gring_v)
        self.xsring = Ring(T, "xs", self.xs_v)
        self.junkb = T.buf("junk")
        self.statb = T.buf("stat")
        for b in self.persist:
            b.w = None
            b.r = {}

    def bank(self, i, n=1):
        return self.ps[:, i * 512:(i + n) * 512]

    def load_x(self, x_dram):
        T = self.T
        xv = x_dram.rearrange("(m p) d -> p m d", p=128)
        for q in range(4):
            T.dma("sp", self.X[:, q * 4:(q + 1) * 4, :], xv[:, q * 4:(q + 1) * 4, :],
                  writes=self.Xb[q * 4:(q + 1) * 4])

    def rms_stats(self, chunks, col0, scr0=32):
        T = self.T
        n = len(chunks)
        ss = self.stat[:, scr0:scr0 + n]
        T.op("dve", lambda e: e.memset(self.stat[:, scr0:scr0 + 2 * n], 0.0), writes=[self.statb])
        for i, m in enumerate(chunks):
            T.op("act", lambda e, i=i, m=m: e.activation(out=self.junk, in_=self.X[:, m, :], func=AF.Square,
                                                         accum_out=self.stat[:, scr0 + i:scr0 + i + 1]),
                 reads=[self.Xb[m]], writes=[self.junkb, self.statb])
        ms = self.stat[:, scr0 + n:scr0 + 2 * n]
        T.op("dve", lambda e: e.tensor_scalar(out=ms, in0=ss, scalar1=1.0 / D, scalar2=NORM_EPS,
                                              op0=ALU.mult, op1=ALU.add), reads=[self.statb], writes=[self.statb])
        T.op("act", lambda e: e.activation(out=ms, in_=ms, func=AF.Ln), reads=[self.statb], writes=[self.statb])
        T.op("act", lambda e: e.activation(out=self.stat[:, col0:col0 + n], in_=ms, func=AF.Exp, scale=-0.5),
             reads=[self.statb], writes=[self.statb])

    def load_gain(self, row):
        T = self.T
        gv, gb = self.gring.next()
        T.dma("sp", gv, self.dr["gains"][row:row + 1, :].broadcast_to([128, D]), writes=[gb])
        return gv, gb

    def norm_to_ht(self, chunks, gain_row, HT, HTb, tok0, trbanks, gain=None, col0=0, scr0=32):
        T = self.T
        gv, gb = gain if gain is not None else self.load_gain(gain_row)
        self.rms_stats(chunks, col0, scr0)
        for i, m in enumerate(chunks):
            xs, xsb = self.xsring.next()
            T.op("dve", lambda e, i=i, m=m, xs=xs: e.scalar_tensor_tensor(
                out=xs, in0=self.X[:, m, :], scalar=self.stat[:, col0 + i:col0 + i + 1], in1=gv,
                op0=ALU.mult, op1=ALU.mult),
                reads=[self.Xb[m], self.statb, gb], writes=[xsb])
            bk = trbanks[i % len(trbanks)]
            trv = self.bank(bk).bitcast(BF16)
            for kc in range(8):
                T.op("pe", lambda e, kc=kc, xs=xs, trv=trv: e.transpose(
                    trv[:, kc * 128:(kc + 1) * 128], xs[:, kc * 128:(kc + 1) * 128], self.ident),
                    reads=[xsb, self.identb], writes=[self.psb[bk]])
            dst = HT[:, :, tok0 + i * 128: tok0 + (i + 1) * 128]
            src = trv.rearrange("p (k t) -> p k t", t=128)
            if i % 2 == 0:
                T.op("act", lambda e, dst=dst, src=src: e.copy(out=dst, in_=src), reads=[self.psb[bk]], writes=[HTb[i]])
            else:
                T.op("dve", lambda e, dst=dst, src=src: e.tensor_copy(out=dst, in_=src), reads=[self.psb[bk]], writes=[HTb[i]])

    def norm_part_a(self, chunks, gain, col0, scr0, ring):
        T = self.T
        gv, gb = gain
        self.rms_stats(chunks, col0, scr0)
        out = []
        for i, m in enumerate(chunks):
            xs, xsb = ring.next()
            T.op("dve", lambda e, i=i, m=m, xs=xs: e.scalar_tensor_tensor(
                out=xs, in0=self.X[:, m, :], scalar=self.stat[:, col0 + i:col0 + i + 1], in1=gv,
                op0=ALU.mult, op1=ALU.mult),
                reads=[self.Xb[m], self.statb, gb], writes=[xsb])
            out.append((xs, xsb))
        return out

    def norm_part_b(self, xss, HT, HTb, tok0, trbanks):
        T = self.T
        for i, (xs, xsb) in enumerate(xss):
            bk = trbanks[i % len(trbanks)]
            trv = self.bank(bk).bitcast(BF16)
            for kc in range(8):
                T.op("pe", lambda e, kc=kc, xs=xs, trv=trv: e.transpose(
                    trv[:, kc * 128:(kc + 1) * 128], xs[:, kc * 128:(kc + 1) * 128], self.ident),
                    reads=[xsb, self.identb], writes=[self.psb[bk]])
            dst = HT[:, :, tok0 + i * 128: tok0 + (i + 1) * 128]
            src = trv.rearrange("p (k t) -> p k t", t=128)
            if i % 2 == 0:
                T.op("act", lambda e, dst=dst, src=src: e.copy(out=dst, in_=src), reads=[self.psb[bk]], writes=[HTb[i]])
            else:
                T.op("dve", lambda e, dst=dst, src=src: e.tensor_copy(out=dst, in_=src), reads=[self.psb[bk]], writes=[HTb[i]])

    def ffn(self, f_idx, gain_row, final=None, gain=None):
        T, A = self.T, self.A
        A.mark()
        G = 6
        HT = A.bf16(8 * TOK).rearrange("p (k t) -> p k t", t=TOK)
        HTb = T.bufs_n("HT", NCH)
        ACTT = A.bf16(G * TOK).rearrange("p (g t) -> p g t", t=TOK)
        ACTb = [[T.buf("actt%d_%d" % (g, ts)) for ts in range(4)] for g in range(G)]
        WOUT = A.bf16(NFC * D).rearrange("p (i n) -> p i n", n=D)
        WOUTb = T.buf("wout")
        winr = Ring(T, "win", [A.bf16(2048) for _ in range(3)])
        sgr = Ring(T, "sg", [A.f32(512) for _ in range(2)])
        win_d = self.dr["ffn_win"]
        wout_d = self.dr["ffn_wout"]
        wq_bounds = [0, 6, 12, 17, 22]
        wout_loaded = [0]

        def load_wout_piece():
            q = wout_loaded[0]
            if q >= 4:
                return
            wout_loaded[0] += 1
            a, b = wq_bounds[q], wq_bounds[q + 1]
            T.dma("pool", WOUT[:, a:b, :], wout_d[f_idx, :, a * D:b * D].rearrange("p (i n) -> p i n", n=D),
                  writes=[WOUTb])
        loaded = []

        def ensure(i):
            while len(loaded) <= min(i + 2, NFC - 1):
                v, b = winr.next()
                T.dma("pool", v, win_d[f_idx, len(loaded), :, :], writes=[b])
                loaded.append((v, b))
        ensure(0)
        gain = gain if gain is not None else self.load_gain(gain_row)
        normed = set()

        xs4 = Ring(T, "xs4", self.xs_v + [A.bf16(D) for _ in range(2)])
        parts = {}

        def part_a(ts):
            parts[ts] = self.norm_part_a([4 * ts + c for c in range(4)], gain, 4 * ts, 32 + 8 * ts, xs4)

        def part_b(ts):
            normed.add(ts)
            self.norm_part_b(parts[ts], HT, HTb[4 * ts:4 * ts + 4], ts * 512, [6, 7])

        def ensure_ht(ts):
            assert ts in normed
        part_a(0)
        part_b(0)
        part_a(1)
        fin_gain = self.load_gain(final[0]) if final is not None else None
        fin_buf = T.buf("finstore") if final is not None else None
        unit = 0
        groups = [list(range(g0, min(g0 + G, NFC))) for g0 in range(0, NFC, G)]
        ob = 0
        ucount = [0]

        def unit_fn(il, i, ts):
            wv, wb = loaded[i]
            ensure_ht(ts)
            bg, bu = (0, 1) if ucount[0] % 2 == 0 else (2, 3)
            ucount[0] += 1
            rhs_b = HTb[ts * 4:(ts + 1) * 4]
            for which, bk in ((0, bg), (1, bu)):
                for kc in range(8):
                    T.op("pe", lambda e, kc=kc, which=which, bk=bk: e.matmul(
                        self.bank(bk), lhsT=wv[:, kc * 256 + which * 128: kc * 256 + which * 128 + 128],
                        rhs=HT[:, kc, ts * 512:(ts + 1) * 512], start=(kc == 0), stop=(kc == 7)),
                        reads=[wb] + rhs_b, writes=[self.psb[bk]])
            sg, sgb = sgr.next()
            T.op("act", lambda e: e.activation(out=sg, in_=self.bank(bg), func=AF.Silu),
                 reads=[self.psb[bg]], writes=[sgb])
            T.op("dve", lambda e: e.tensor_tensor(
                out=ACTT[:, il, ts * 512:(ts + 1) * 512], in0=sg, in1=self.bank(bu), op=ALU.mult),
                reads=[sgb, self.psb[bu]], writes=[ACTb[il][ts]])

        for gi, grp in enumerate(groups):
            if gi == 0:
                ensure(0)
                for _ in range(3):
                    load_wout_piece()
                for ts in range(4):
                    for il in range(3):
                        unit_fn(il, grp[il], ts)
                    if ts + 1 < 4:
                        part_b(ts + 1)
                    if ts + 2 < 4:
                        part_a(ts + 2)
                rest = list(enumerate(grp))[3:]
            else:
                rest = list(enumerate(grp))
            for il, i in rest:
                ensure(i)
                load_wout_piece()
                for ts in range(4):
                    unit_fn(il, i, ts)
            for m in range(NCH):
                b0 = 4 if ob % 2 == 0 else 6
                ob += 1
                for n in range(2):
                    for il, i in enumerate(grp):
                        T.op("pe", lambda e, il=il, i=i, n=n, m=m, b0=b0: e.matmul(
                            self.bank(b0 + n), lhsT=ACTT[:, il, m * 128:(m + 1) * 128],
                            rhs=WOUT[:, i, n * 512:(n + 1) * 512], start=(il == 0), stop=(il == len(grp) - 1)),
                            reads=[ACTb[il][m // 4], WOUTb], writes=[self.psb[b0 + n]])
                T.op("dve", lambda e, m=m, b0=b0: e.scalar_tensor_tensor(
                    out=self.X[:, m, :], in0=self.bank(b0, 2), scalar=0.5, in1=self.X[:, m, :],
                    op0=ALU.mult, op1=ALU.add),
                    reads=[self.psb[b0], self.psb[b0 + 1], self.Xb[m]], writes=[self.Xb[m]])
                if final is not None and grp is groups[-1] and m % 4 == 3:
                    q = m // 4
                    cks = [4 * q + c for c in range(4)]
                    self.rms_stats(cks, 16 + 4 * q, 32 + 8 * q)
                    gv, gb = fin_gain
                    for mm in cks:
                        T.op("dve", lambda e, mm=mm, q=q: e.scalar_tensor_tensor(
                            out=self.X[:, mm, :], in0=self.X[:, mm, :],
                            scalar=self.stat[:, 16 + mm:17 + mm], in1=gv, op0=ALU.mult, op1=ALU.mult),
                            reads=[self.Xb[mm], self.statb, gb], writes=[self.Xb[mm]])
                    ov = final[1].rearrange("(m p) d -> p m d", p=128)
                    T.dma("sp", ov[:, q * 4:(q + 1) * 4, :], self.X[:, q * 4:(q + 1) * 4, :],
                          reads=self.Xb[q * 4:(q + 1) * 4], sembuf=fin_buf)
        self.phase_barrier()
        A.release()

    def store_x(self, out_dram, gain_row=None):
        T, A = self.T, self.A
        A.mark()
        ov = out_dram.rearrange("(m p) d -> p m d", p=128)
        stb = T.buf("store")
        if gain_row is None:
            for q in range(4):
                T.dma("sp", ov[:, q * 4:(q + 1) * 4, :], self.X[:, q * 4:(q + 1) * 4, :],
                      reads=self.Xb[q * 4:(q + 1) * 4], sembuf=stb)
        else:
            gv, gb = self.load_gain(gain_row)
            self.rms_stats(list(range(NCH)), 0)
            for m in range(NCH):
                T.op("dve", lambda e, m=m: e.scalar_tensor_tensor(
                    out=self.X[:, m, :], in0=self.X[:, m, :], scalar=self.stat[:, m:m + 1], in1=gv,
                    op0=ALU.mult, op1=ALU.mult), reads=[self.Xb[m], self.statb, gb], writes=[self.Xb[m]])
                if m % 4 == 3:
                    q = m // 4
                    T.dma("sp", ov[:, q * 4:(q + 1) * 4, :], self.X[:, q * 4:(q + 1) * 4, :],
                          reads=self.Xb[q * 4:(q + 1) * 4], sembuf=stb)
        self.phase_barrier()
        A.release()

    def sgu(self):
        T, A, d = self.T, self.A, self.dr
        A.mark()
        AX = mybir.AxisListType
        GELU = AF.Gelu_apprx_tanh
        wsT_f = A.f32(1024)
        wsTb = T.buf("wsT")
        T.dma("sp", wsT_f, d["a_wsT"][:, :], writes=[wsTb])
        w3 = wsT_f.rearrange("p (g t) -> p g t", t=128)
        T.op("pool", lambda e: e.affine_select(out=w3, in_=w3, pattern=[[0, 8], [1, 128]], compare_op=ALU.is_ge,
                                               fill=0.0, base=0, channel_multiplier=-1), reads=[wsTb], writes=[wsTb])
        wsT_bf = A.bf16(1024)
        wsbfb = T.buf("wsbf")
        T.op("dve", lambda e: e.tensor_copy(out=wsT_bf, in_=wsT_f), reads=[wsTb], writes=[wsbfb])
        BS = A.f32(1024)
        BSb = T.buf("bs")
        T.dma("sp", BS, d["a_bs"][0:1, :].broadcast_to([128, 1024]), writes=[BSb])
        lng = A.f32(16)
        lnb = A.f32(16)
        lnbuf = T.buf("ln")
        T.dma("sp", lng, d["a_lng"][:, :], writes=[lnbuf])
        T.dma("sp", lnb, d["a_lnb"][:, :], writes=[lnbuf])
        T1 = A.f32(16 * 128).rearrange("p (f t) -> p f t", t=128)
        T1b = T.buf("T1")
        for g in range(8):
            T.op("pe", lambda e, g=g: e.matmul(self.ps[:, g * 128:(g + 1) * 128], lhsT=self.ones_bf,
                                                rhs=wsT_bf[:, g * 128:(g + 1) * 128], start=True, stop=True),
                 reads=[self.onesb, wsbfb], writes=[self.psb[g // 4]])
        for fc in range(16):
            g = fc // 2
            T.op("dve", lambda e, fc=fc, g=g: e.scalar_tensor_tensor(
                out=T1[:, fc, :], in0=self.ps[:, g * 128:(g + 1) * 128], scalar=lnb[:, fc:fc + 1],
                in1=BS[:, g * 128:(g + 1) * 128], op0=ALU.mult, op1=ALU.add),
                reads=[self.psb[g // 4], lnbuf, BSb], writes=[T1b])
        HTt = A.bf16(8 * 512).rearrange("p (k t) -> p k t", t=512)
        HTtb = T.bufs_n("HTt", 4)
        VRAW = A.bf16(4 * 2048).rearrange("p (c f) -> p c f", f=2048)
        VRAWb = T.bufs_n("vraw", 4)
        vtr = Ring(T, "vt", [A.f32(256) for _ in range(3)])
        UVT = A.bf16(16 * 512).rearrange("p (f t) -> p f t", t=512)
        UVTb = T.bufs_n("uvt", 16)
        awvr = Ring(T, "awv", [A.bf16(2048) for _ in range(3)])
        awur = Ring(T, "awu", [A.bf16(1024) for _ in range(3)])
        awor = Ring(T, "awo", [A.bf16(1024) for _ in range(8)])
        usbr = Ring(T, "usb", [A.f32(512) for _ in range(2)])
        tmpr = Ring(T, "tmp", [A.f32(512) for _ in range(2)])
        wsTp = [A.bf16(1024) for _ in range(4)]
        wsTpb = T.bufs_n("wsTp", 4)
        NEGMU = [A.bf16(128) for _ in range(4)]
        NEGMUb = T.bufs_n("negmu", 4)
        st = A.f32(128)
        stb = T.buf("sgst")
        SV = st[:, 0:32].rearrange("p (c f) -> p c f", f=8)
        SQ = st[:, 32:64].rearrange("p (c f) -> p c f", f=8)
        S1, S2, MEAN, MSQ, VAR, RSTD, NMU = (st[:, 64 + 4 * i:68 + 4 * i] for i in range(7))
        vb = 0
        gain1 = self.load_gain(1)
        xs4 = Ring(T, "xs4", self.xs_v + [A.bf16(D) for _ in range(2)])

        def sgu_a(tt):
            return self.norm_part_a([4 * tt + c for c in range(4)], gain1, 4 * tt, 32 + 8 * tt, xs4)
        self.norm_part_b(sgu_a(0), HTt, HTtb, 0, [6, 7])
        for tt in range(4):
            chunks = [4 * tt + c for c in range(4)]
            T.op("dve", lambda e: e.memset(st[:, 0:64], 0.0), writes=[stb])
            for ft in range(8):
                av, avb = awvr.next()
                T.dma("pool", av, d["a_win_v"][ft, :, :], writes=[avb])
                for c in range(4):
                    bk = vb % 4
                    vb += 1
                    pv = self.ps[:, bk * 512: bk * 512 + 256]
                    for kc in range(8):
                        T.op("pe", lambda e, kc=kc, c=c, av=av, pv=pv: e.matmul(
                            pv, lhsT=HTt[:, kc, c * 128:(c + 1) * 128], rhs=av[:, kc * 256:(kc + 1) * 256],
                            start=(kc == 0), stop=(kc == 7)), reads=[HTtb[c], avb], writes=[self.psb[bk]])
                    vt, vtb = vtr.next()
                    T.op("act", lambda e, vt=vt, pv=pv, c=c, ft=ft: e.activation(
                        out=vt, in_=pv, func=GELU, accum_out=SV[:, c, ft:ft + 1]),
                        reads=[self.psb[bk]], writes=[vtb, stb])
                    T.op("act", lambda e, vt=vt, c=c, ft=ft: e.activation(
                        out=self.junk[:, 0:256], in_=vt, func=AF.Square, accum_out=SQ[:, c, ft:ft + 1]),
                        reads=[vtb], writes=[self.junkb, stb])
                    T.op("dve", lambda e, vt=vt, c=c, ft=ft: e.tensor_copy(
                        out=VRAW[:, c, ft * 256:(ft + 1) * 256], in_=vt), reads=[vtb], writes=[VRAWb[c]])
            T.op("dve", lambda e: e.tensor_reduce(out=S1, in_=SV, axis=AX.X, op=ALU.add), reads=[stb], writes=[stb])
            T.op("dve", lambda e: e.tensor_reduce(out=S2, in_=SQ, axis=AX.X, op=ALU.add), reads=[stb], writes=[stb])
            T.op("dve", lambda e: e.tensor_scalar(out=MEAN, in0=S1, scalar1=1.0 / 2048, scalar2=None, op0=ALU.mult),
                 reads=[stb], writes=[stb])
            T.op("dve", lambda e: e.tensor_tensor(out=MSQ, in0=MEAN, in1=MEAN, op=ALU.mult), reads=[stb], writes=[stb])
            T.op("dve", lambda e: e.scalar_tensor_tensor(out=VAR, in0=S2, scalar=1.0 / 2048, in1=MSQ,
                                                         op0=ALU.mult, op1=ALU.subtract), reads=[stb], writes=[stb])
            T.op("dve", lambda e: e.tensor_scalar(out=VAR, in0=VAR, scalar1=LN_EPS, scalar2=None, op0=ALU.add),
                 reads=[stb], writes=[stb])
            T.op("act", lambda e: e.activation(out=VAR, in_=VAR, func=AF.Ln), reads=[stb], writes=[stb])
            T.op("act", lambda e: e.activation(out=RSTD, in_=VAR, func=AF.Exp, scale=-0.5), reads=[stb], writes=[stb])
            T.op("dve", lambda e: e.tensor_scalar(out=NMU, in0=MEAN, scalar1=-1.0, scalar2=None, op0=ALU.mult),
                 reads=[stb], writes=[stb])
            for c in range(4):
                T.op("dve", lambda e, c=c: e.tensor_scalar(out=wsTp[c], in0=wsT_f, scalar1=RSTD[:, c:c + 1],
                                                           scalar2=None, op0=ALU.mult),
                     reads=[wsTb, stb], writes=[wsTpb[c]])
                T.op("dve", lambda e, c=c: e.tensor_scalar(out=NEGMU[c], in0=self.ones_bf, scalar1=NMU[:, c:c + 1],
                                                           scalar2=None, op0=ALU.mult),
                     reads=[self.onesb, stb], writes=[NEGMUb[c]])
            aw_first = []
            for fcl in range(8):
                v_, b_ = awor.next()
                T.dma("pool", v_, d["a_wout"][:, fcl * D:(fcl + 1) * D], writes=[b_])
                aw_first.append((v_, b_))
            for fc in range(16):
                g = fc // 2
                au, aub = awur.next()
                T.dma("pool", au, d["a_win_u"][fc, :, :], writes=[aub])
                bu = 4 + fc % 2
                bs_ = 6 + fc % 2
                for kc in range(8):
                    T.op("pe", lambda e, kc=kc, au=au, bu=bu: e.matmul(
                        self.bank(bu), lhsT=au[:, kc * 128:(kc + 1) * 128], rhs=HTt[:, kc, :],
                        start=(kc == 0), stop=(kc == 7)), reads=[aub] + HTtb, writes=[self.psb[bu]])
                usb, usbb = usbr.next()
                T.op("act", lambda e, usb=usb, bu=bu: e.activation(out=usb, in_=self.bank(bu), func=GELU),
                     reads=[self.psb[bu]], writes=[usbb])
                for c in range(4):
                    po = self.ps[:, bs_ * 512 + c * 128: bs_ * 512 + (c + 1) * 128]
                    T.op("pe", lambda e, c=c, fc=fc, g=g, po=po: e.matmul(
                        po, lhsT=VRAW[:, c, fc * 128:(fc + 1) * 128], rhs=wsTp[c][:, g * 128:(g + 1) * 128],
                        start=True, stop=False), reads=[VRAWb[c], wsTpb[c]], writes=[self.psb[bs_]])
                    T.op("pe", lambda e, c=c, g=g, po=po: e.matmul(
                        po, lhsT=NEGMU[c], rhs=wsTp[c][:, g * 128:(g + 1) * 128], start=False, stop=True),
                        reads=[NEGMUb[c], wsTpb[c]], writes=[self.psb[bs_]])
                tmp, tmpb = tmpr.next()
                T.op("dve", lambda e, tmp=tmp, fc=fc, bs_=bs_: e.scalar_tensor_tensor(
                    out=tmp.rearrange("p (c t) -> p c t", t=128),
                    in0=self.bank(bs_).rearrange("p (c t) -> p c t", t=128), scalar=lng[:, fc:fc + 1],
                    in1=T1[:, fc:fc + 1, :].broadcast_to([128, 4, 128]), op0=ALU.mult, op1=ALU.add),
                    reads=[self.psb[bs_], lnbuf, T1b], writes=[tmpb])
                T.op("dve", lambda e, tmp=tmp, usb=usb, fc=fc: e.tensor_tensor(
                    out=UVT[:, fc, :], in0=tmp, in1=usb, op=ALU.mult), reads=[tmpb, usbb], writes=[UVTb[fc]])
            nxt = sgu_a(tt + 1) if tt + 1 < 4 else None
            ob = 0
            for hf in range(2):
                aw = aw_first if hf == 0 else []
                for fcl in range(8 if hf == 1 else 0):
                    v_, b_ = awor.next()
                    fc = hf * 8 + fcl
                    T.dma("pool", v_, d["a_wout"][:, fc * D:(fc + 1) * D], writes=[b_])
                    aw.append((v_, b_))
                for c in range(4):
                    b0 = 0 if ob % 2 == 0 else 2
                    ob += 1
                    for n in range(2):
                        for fcl in range(8):
                            fc = hf * 8 + fcl
                            T.op("pe", lambda e, fc=fc, fcl=fcl, c=c, n=n, b0=b0: e.matmul(
                                self.bank(b0 + n), lhsT=UVT[:, fc, c * 128:(c + 1) * 128],
                                rhs=aw[fcl][0][:, n * 512:(n + 1) * 512], start=(fcl == 0), stop=(fcl == 7)),
                                reads=[UVTb[fc], aw[fcl][1]], writes=[self.psb[b0 + n]])
                    m = chunks[c]
                    T.op("dve", lambda e, m=m, b0=b0: e.tensor_tensor(
                        out=self.X[:, m, :], in0=self.bank(b0, 2), in1=self.X[:, m, :], op=ALU.add),
                        reads=[self.psb[b0], self.psb[b0 + 1], self.Xb[m]], writes=[self.Xb[m]])
            if nxt is not None:
                self.norm_part_b(nxt, HTt, HTtb, 0, [6, 7])
        self.phase_barrier()
        A.release()

    def kv_exchange(self):
        T, A, d = self.T, self.A, self.dr
        nc = self.nc
        A.mark()
        HT = A.bf16(8 * TOK).rearrange("p (k t) -> p k t", t=TOK)
        HTb = T.bufs_n("HT", NCH)
        wk = A.bf16(2048)
        wv = A.bf16(2048)
        wkb, wvb = T.buf("wk"), T.buf("wv")
        T.dma("pool", wk, d["w_kv_k"][:, :], writes=[wkb])
        T.dma("pool", wv, d["w_kv_v"][:, :], writes=[wvb])
        KVS = A.bf16(8192)
        KVSb = T.buf("kvs")
        gain3 = self.load_gain(3)
        u = 0
        xs4 = Ring(T, "xs4", self.xs_v + [A.bf16(D) for _ in range(2)])
        kparts = {}

        def kv_a(ts):
            kparts[ts] = self.norm_part_a([4 * ts + c for c in range(4)], gain3, 4 * ts, 32 + 8 * ts, xs4)

        def kv_b(ts):
            self.norm_part_b(kparts[ts], HT, HTb[4 * ts:4 * ts + 4], ts * 512, [6, 7])
        kv_a(0)
        kv_b(0)
        kv_a(1)
        for ts in range(4):
            if ts > 0:
                pass
            for jj in range(2):
                bk = u % 4
                u += 1
                for kc in range(8):
                    T.op("pe", lambda e, kc=kc, jj=jj, ts=ts, bk=bk: e.matmul(
                        self.bank(bk), lhsT=wk[:, kc * 256 + jj * 128: kc * 256 + jj * 128 + 128],
                        rhs=HT[:, kc, ts * 512:(ts + 1) * 512], start=(kc == 0), stop=(kc == 7)),
                        reads=[wkb] + HTb[ts * 4:(ts + 1) * 4], writes=[self.psb[bk]])
                dst = KVS[:, jj * TOK + ts * 512: jj * TOK + (ts + 1) * 512]
                if u % 2 == 0:
                    T.op("dve", lambda e, dst=dst, bk=bk: e.tensor_copy(out=dst, in_=self.bank(bk)),
                         reads=[self.psb[bk]], writes=[KVSb])
                else:
                    T.op("act", lambda e, dst=dst, bk=bk: e.copy(out=dst, in_=self.bank(bk)),
                         reads=[self.psb[bk]], writes=[KVSb])
            for m in range(4 * ts, 4 * ts + 4):
                bk = u % 4
                u += 1
                pv = self.ps[:, bk * 512: bk * 512 + 256]
                for kc in range(8):
                    T.op("pe", lambda e, kc=kc, m=m, pv=pv: e.matmul(
                        pv, lhsT=HT[:, kc, m * 128:(m + 1) * 128], rhs=wv[:, kc * 256:(kc + 1) * 256],
                        start=(kc == 0), stop=(kc == 7)), reads=[wvb, HTb[m]], writes=[self.psb[bk]])
                dst = KVS[:, 4096 + m * 256: 4096 + (m + 1) * 256]
                if u % 2 == 0:
                    T.op("dve", lambda e, dst=dst, pv=pv: e.tensor_copy(out=dst, in_=pv), reads=[self.psb[bk]], writes=[KVSb])
                else:
                    T.op("act", lambda e, dst=dst, pv=pv: e.copy(out=dst, in_=pv), reads=[self.psb[bk]], writes=[KVSb])
            if ts + 1 < 4:
                kv_b(ts + 1)
            if ts + 2 < 4:
                kv_a(ts + 2)
        bounce = nc.dram_tensor("kv_bounce", [128, 4096], F32)
        self.gathered = nc.dram_tensor("kv_gathered", [256, 4096], F32)
        bb = T.buf("bounce")
        gb = T.buf("gathered")
        T.dma("sp", bounce[:, :], KVS.bitcast(F32), reads=[KVSb], writes=[bb])
        T.dma("pool", None, None, reads=[bb], writes=[gb], inc=1,
              fn=lambda e: e.collective_compute("AllGather", ALU.bypass,
                                                replica_groups=[[0, 1], [2, 3], [4, 5], [6, 7]],
                                                ins=[bounce.ap().opt()], outs=[self.gathered.ap().opt()]))
        self.phase_barrier()
        A.release()

    def attention(self):
        T, A, d = self.T, self.A, self.dr
        A.mark()
        gain5 = self.load_gain(5)
        KVr = [A.bf16(8192) for _ in range(2)]
        KTb = T.buf("KT")
        Vb = KTb
        gat = self.gathered.ap()
        WQ = A.bf16(8 * D).rearrange("p (k n) -> p k n", n=D)
        WO = A.bf16(8 * D).rearrange("p (k n) -> p k n", n=D)
        WQb, WOb = T.buf("WQ"), T.buf("WO")
        T.dma("pool", WQ, d["w_q"][:, :].rearrange("p (k n) -> p k n", n=D), writes=[WQb])
        for h in range(2):
            T.dma("sp", KVr[h].bitcast(F32), gat[h * 128:(h + 1) * 128, :], writes=[KTb])

        def KTblk(r0, jj, kb):
            h, jl_ = chunk_owner(kb)
            return KVr[h][r0:r0 + 64, jj * TOK + jl_ * 128: jj * TOK + (jl_ + 1) * 128]

        def Vblk(kb, k):
            h, jl_ = chunk_owner(kb)
            return KVr[h][:, 4096 + jl_ * 256 + k * 64: 4096 + jl_ * 256 + (k + 1) * 64]
        MASK = A.bf16(4 * 512)
        MASKb = T.buf("mask")
        T.dma("pool", MASK, d["masks"][:, :], writes=[MASKb])
        T.dma("pool", WO, d["w_o"][:, :].rearrange("p (k n) -> p k n", n=D), writes=[WOb])
        tf = self.junk.bitcast(F32)[:, 0:128]
        tfb = self.junkb
        TRI = A.bf16(128)
        COTRI = A.bf16(128)
        TRIb, COTRIb = T.buf("tri"), T.buf("cotri")
        T.op("pool", lambda e: e.memset(tf, 1.0), writes=[tfb])
        T.op("pool", lambda e: e.affine_select(out=tf, in_=tf, pattern=[[-1, 128]], compare_op=ALU.is_ge, fill=0.0,
                                               base=0, channel_multiplier=1), reads=[tfb], writes=[tfb])
        T.op("dve", lambda e: e.tensor_copy(out=TRI, in_=tf), reads=[tfb], writes=[TRIb])
        T.op("pool", lambda e: e.memset(tf, 1.0), reads=[tfb], writes=[tfb])
        T.op("pool", lambda e: e.affine_select(out=tf, in_=tf, pattern=[[1, 128]], compare_op=ALU.is_gt, fill=0.0,
                                               base=0, channel_multiplier=-1), reads=[tfb], writes=[tfb])
        T.op("dve", lambda e: e.tensor_copy(out=COTRI, in_=tf), reads=[tfb], writes=[COTRIb])
        HTq = A.bf16(8 * 512).rearrange("p (k t) -> p k t", t=512)
        HTqb = T.bufs_n("HTq", 4)
        QT2 = [A.bf16(8 * 512).rearrange("p (h t) -> p h t", t=512) for _ in range(2)]
        QTb2 = [T.bufs_n("QT%d_" % i, 4) for i in range(2)]
        otr = Ring(T, "OT", [A.bf16(8 * 128).rearrange("p (h t) -> p h t", t=128) for _ in range(2)])
        Er = Ring(T, "E", [A.f32(1024) for _ in range(3)])
        SPr = Ring(T, "SP", [A.bf16(1024) for _ in range(4)])
        Wr = Ring(T, "W", [A.f32(1024) for _ in range(2)])
        Ar = Ring(T, "A", [A.bf16(1024) for _ in range(2)])
        zpairs = [(0, 1)]
        PB = 4
        OB = 6
        zi = [0]

        def stage_a(u):
            pp, kb, jl, j, kmax = u["pp"], u["kb"], u["jl"], u["j"], u["kmax"]
            QT, QTb = u["QT"]
            zb = zpairs[zi[0] % len(zpairs)]
            zi[0] += 1
            for c in range(2):
                k = 2 * pp + c
                r0 = c * 64
                T.op("pe", lambda e, c=c, r0=r0: e.matmul(
                    self.bank(zb[c]), lhsT=KTblk(r0, pp, kb),
                    rhs=QT[r0:r0 + 64, pp * 4:pp * 4 + 4, jl * 128:(jl + 1) * 128], start=True, stop=True),
                    reads=[KTb, QTb[k]], writes=[self.psb[zb[c]]])
            zbufs = [self.psb[zb[0]], self.psb[zb[1]]]
            E, Eb = Er.next()
            T.op("act", lambda e: e.activation(out=E, in_=self.bank(zb[0], 2), func=AF.Exp), reads=zbufs, writes=[Eb])
            SP, SPb = SPr.next()
            special = kb >= kmax - 1
            mk = None
            if special:
                mi = (j % 2) * 2 + (0 if kb == kmax else 1)
                mk = MASK[:, mi * 512:(mi + 1) * 512]
                SPf, SPfb = Wr.next()
                T.op("act", lambda e: e.activation(out=SPf, in_=E, func=AF.Ln, bias=1.0), reads=[Eb], writes=[SPfb])
                for c in range(2):
                    T.op("dve", lambda e, c=c: e.tensor_tensor(out=SP[:, c * 512:(c + 1) * 512],
                                                               in0=SPf[:, c * 512:(c + 1) * 512], in1=mk, op=ALU.mult),
                         reads=[SPfb, MASKb], writes=[SPb])
            else:
                T.op("act", lambda e: e.activation(out=SP, in_=E, func=AF.Ln, bias=1.0), reads=[Eb], writes=[SPb])
            u.update(E=E, Eb=Eb, SP=SP, SPb=SPb, mk=mk)

        def stage_b(u):
            ch = u["chain"]
            pbufs = [self.psb[PB], self.psb[PB + 1]]
            for c in range(2):
                sl = slice(c * 512, (c + 1) * 512)
                if u["first"]:
                    T.op("pe", lambda e, c=c, sl=sl: e.matmul(self.bank(PB + c), lhsT=TRI, rhs=u["SP"][:, sl],
                                                              start=True, stop=True),
                         reads=[TRIb, u["SPb"]], writes=[pbufs[c]])
                else:
                    pv = ch["prev"]
                    T.op("pe", lambda e, c=c, sl=sl, pv=pv: e.matmul(
                        self.bank(PB + c), lhsT=COTRI, rhs=pv["SP"][:, sl], start=False, stop=False,
                        skip_group_check=True), reads=[COTRIb, pv["SPb"]], writes=[pbufs[c]])
                    T.op("pe", lambda e, c=c, sl=sl: e.matmul(
                        self.bank(PB + c), lhsT=TRI, rhs=u["SP"][:, sl], start=False, stop=True,
                        skip_group_check=True), reads=[TRIb, u["SPb"]], writes=[pbufs[c]])
            ch["prev"] = u
            W, Wb = Wr.next()
            T.op("act", lambda e: e.activation(out=W, in_=self.bank(PB, 2), func=AF.Exp, scale=-1.0),
                 reads=pbufs, writes=[Wb])
            Av, Ab = Ar.next()
            T.op("dve", lambda e: e.tensor_tensor(out=Av, in0=u["E"], in1=W, op=ALU.mult),
                 reads=[u["Eb"], Wb], writes=[Ab])
            if u["mk"] is not None:
                for c in range(2):
                    T.op("dve", lambda e, c=c, Av=Av: e.tensor_tensor(out=Av[:, c * 512:(c + 1) * 512],
                                                                      in0=Av[:, c * 512:(c + 1) * 512], in1=u["mk"],
                                                                      op=ALU.mult),
                         reads=[Ab, MASKb], writes=[Ab])
            u.update(A=Av, Ab=Ab)

        def stage_c(u):
            pp, kb = u["pp"], u["kb"]
            OB = u["chain"]["ob"]
            for c in range(2):
                k = 2 * pp + c
                a3 = u["A"][:, c * 512:(c + 1) * 512].rearrange("p (h t) -> p h t", t=128)
                for hh in range(2):
                    T.op("pe", lambda e, c=c, k=k, hh=hh, a3=a3: e.matmul(
                        self.ps[hh * 64:(hh + 1) * 64, OB * 512 + c * 256: OB * 512 + (c + 1) * 256],
                        lhsT=Vblk(kb, k), rhs=a3[:, hh::2, :],
                        start=(u["first"] and c == 0), stop=u["last"], skip_group_check=True),
                        reads=[Vb, u["Ab"]], writes=[self.psb[OB]])
            if u["last"]:
                OT, OTb = u["OT"]
                T.op("dve", lambda e: e.tensor_copy(
                    out=OT[:, 4 * pp:4 * pp + 4, :],
                    in_=self.bank(OB).rearrange("p (h t) -> p h t", t=128)),
                    reads=[self.psb[OB]], writes=[OTb])

        def out_proj(j, OTv, n_):
            OT, OTb = OTv
            for hp in range(8):
                T.op("pe", lambda e, hp=hp: e.matmul(
                    self.bank(3), lhsT=OT[:, hp, :], rhs=WO[:, hp, n_ * 512:(n_ + 1) * 512],
                    start=(hp == 0), stop=(hp == 7)), reads=[OTb, WOb], writes=[self.psb[3]])
            T.op("dve", lambda e: e.tensor_tensor(
                out=self.X[:, j, n_ * 512:(n_ + 1) * 512], in0=self.bank(3),
                in1=self.X[:, j, n_ * 512:(n_ + 1) * 512], op=ALU.add),
                reads=[self.psb[3], self.Xb[j]], writes=[self.Xb[j]])

        def qproj_slices(qb):
            chunks = [4 * qb + c for c in range(4)]
            QT, QTb = QT2[qb % 2], QTb2[qb % 2]
            gv, gb = gain5
            col0, scr0 = 4 * qb, 32 + 8 * qb
            sl = [lambda: self.rms_stats(chunks, col0, scr0)]

            def norm_chunk(i, m):
                xs, xsb = self.xsring.next()
                T.op("dve", lambda e: e.scalar_tensor_tensor(
                    out=xs, in0=self.X[:, m, :], scalar=self.stat[:, col0 + i:col0 + i + 1], in1=gv,
                    op0=ALU.mult, op1=ALU.mult), reads=[self.Xb[m], self.statb, gb], writes=[xsb])
                trv = self.bank(7).bitcast(BF16)
                for kc in range(8):
                    T.op("pe", lambda e, kc=kc: e.transpose(
                        trv[:, kc * 128:(kc + 1) * 128], xs[:, kc * 128:(kc + 1) * 128], self.ident),
                        reads=[xsb, self.identb], writes=[self.psb[7]])
                T.op("dve", lambda e: e.tensor_copy(out=HTq[:, :, i * 128:(i + 1) * 128],
                                                    in_=trv.rearrange("p (k t) -> p k t", t=128)),
                     reads=[self.psb[7]], writes=[HTqb[i]])
            for i, m in enumerate(chunks):
                sl.append(lambda i=i, m=m: norm_chunk(i, m))

            def qhead(h):
                k = h // 4
                r0 = (k % 2) * 64
                slot = (k // 2) * 4 + h % 4
                pq = self.ps[r0:r0 + 64, 7 * 512:8 * 512]
                for kc in range(8):
                    T.op("pe", lambda e, kc=kc: e.matmul(
                        pq, lhsT=WQ[:, kc, h * 64:(h + 1) * 64], rhs=HTq[:, kc, :], start=(kc == 0), stop=(kc == 7)),
                        reads=[WQb] + HTqb, writes=[self.psb[7]])
                T.op("dve", lambda e: e.tensor_scalar(
                    out=QT[r0:r0 + 64, slot, :], in0=pq, scalar1=0.125, scalar2=None, op0=ALU.mult),
                    reads=[self.psb[7]], writes=[QTb[k]])
            for h in range(16):
                sl.append(lambda h=h: qhead(h))
            return sl

        units = []
        nchain = [0]
        oproj = []
        batch_first_unit = {}
        for qb in range(4):
            batch_first_unit[qb] = len(units)
            for jl in range(4):
                j = 4 * qb + jl
                kmax = 4 * (j // 2) + (1 if j % 2 == 0 else 3)
                OTv = otr.next()
                for pp in range(2):
                    chain = dict(prev=None, ob=(6 if nchain[0] % 2 == 0 else 2))
                    nchain[0] += 1
                    for kb in range(kmax, -1, -1):
                        units.append(dict(pp=pp, kb=kb, jl=jl, j=j, kmax=kmax, first=(kb == kmax), last=(kb == 0),
                                          chain=chain, OT=OTv, QT=(QT2[qb % 2], QTb2[qb % 2]), pre=[], post=None))
                oproj.append((len(units) - 1, j, OTv))
        sl0 = qproj_slices(0)
        for f in sl0[:13]:
            f()
        late0 = sl0[13:]
        for qb in range(3):
            sl = qproj_slices(qb + 1)
            u0 = batch_first_unit[qb]
            nb = batch_first_unit[qb + 1] - u0
            assert nb >= len(sl)
            for i, f in enumerate(sl):
                units[u0 + i]["pre"].append(f)
        units[0]["pre"] = late0[:4] + units[0]["pre"]
        units[1]["pre"] = late0[4:] + units[1]["pre"]
        assert units[2]["pp"] == 1 and units[1]["pp"] == 0
        n = len(units)
        posts = {}
        for (ui, j, OTv) in oproj:
            for n_ in range(2):
                posts.setdefault(ui + 2 + 2 * n_, []).append((lambda j=j, OTv=OTv, n_=n_: out_proj(j, OTv, n_)))
        for t in range(n + 2):
            if t < n:
                for f in units[t]["pre"]:
                    f()
                stage_a(units[t])
            if 1 <= t <= n:
                stage_b(units[t - 1])
            if t >= 2:
                stage_c(units[t - 2])
                for f in posts.pop(t - 2, []):
                    f()
        for k_ in sorted(posts):
            for f in posts[k_]:
                f()
        self.phase_barrier()
        A.release()


def build(mode="FUSED", stages=None):
    P = Prog(mode)
    P.din("x", [TOK, D])
    P.din("gains", [8, D])
    P.din("ffn_win", [4, NFC, 128, 2048])
    P.din("ffn_wout", [4, 128, NFC * D])
    P.din("a_wsT", [128, 1024])
    P.din("a_bs", [1, 1024])
    P.din("a_lng", [128, 16])
    P.din("a_lnb", [128, 16])
    P.din("a_win_v", [8, 128, 2048])
    P.din("a_win_u", [16, 128, 1024])
    P.din("a_wout", [128, 16 * D])
    P.din("w_kv_k", [128, 2048])
    P.din("w_kv_v", [128, 2048])
    P.din("w_q", [128, 8 * D])
    P.din("w_o", [128, 8 * D])
    P.din("masks", [128, 4 * 512])
    P.dout("y", [TOK, D])
    es = contextlib.ExitStack()
    with es:
        P.setup(es)
        g0 = P.load_gain(0)
        P.load_x(P.dr["x"])
        st = stages or ["ffn0a", "sgu", "ffn0b", "kv", "ffn1a", "attn", "ffn1b", "final"]
        if "ffn0a" in st:
            P.ffn(0, 0, gain=g0)
        if "sgu" in st:
            P.sgu()
        if "ffn0b" in st:
            P.ffn(1, 2)
        if "kv" in st:
            P.kv_exchange()
        if "ffn1a" in st:
            P.ffn(2, 4)
        if "attn" in st:
            P.attention()
        if "ffn1b" in st and "final" in st:
            P.ffn(3, 6, final=(7, P.dr["y"]))
        else:
            if "ffn1b" in st:
                P.ffn(3, 6)
            P.store_x(P.dr["y"], gain_row=(7 if "final" in st else None))
    return P.nc


def prep_weights(inp):
    f = np.float32
    w = {}
    win = np.asarray(inp["ffn_w_in"], f).reshape(4, 8, 128, 2, NFC, 128)
    w["ffn_win"] = np.ascontiguousarray(win.transpose(0, 4, 2, 1, 3, 5)).reshape(4, NFC, 128, 2048)
    wout = np.asarray(inp["ffn_w_out"], f).reshape(4, NFC, 128, D)
    w["ffn_wout"] = np.ascontiguousarray(wout.transpose(0, 2, 1, 3)).reshape(4, 128, NFC * D)
    g = np.zeros((8, D), f)
    fn = np.asarray(inp["ffn_norm"], f)
    mn = np.asarray(inp["mix_norm"], f)
    g[0], g[1], g[2], g[3] = fn[0, 0], mn[0], fn[0, 1], np.asarray(inp["kv_norm"], f)
    g[4], g[5], g[6], g[7] = fn[1, 0], mn[1], fn[1, 1], np.asarray(inp["final_norm"], f)
    w["gains"] = g
    return w


def prep_l0(inp):
    f = np.float32
    w = {}
    w["a_wsT"] = np.ascontiguousarray(np.asarray(inp["a_w_s"], f)[0].transpose(2, 0, 1)).reshape(128, 1024)
    w["a_bs"] = np.ascontiguousarray(np.asarray(inp["a_b_s"], f)[0]).reshape(1, 1024)
    w["a_lng"] = np.ascontiguousarray(np.asarray(inp["a_ln_g"], f)[0].reshape(16, 128).T)
    w["a_lnb"] = np.ascontiguousarray(np.asarray(inp["a_ln_b"], f)[0].reshape(16, 128).T)
    awin = np.asarray(inp["a_w_in"], f)[0]
    w["a_win_v"] = np.ascontiguousarray(awin[:, 2048:].reshape(8, 128, 8, 256).transpose(2, 1, 0, 3)).reshape(8, 128, 2048)
    w["a_win_u"] = np.ascontiguousarray(awin[:, :2048].reshape(8, 128, 16, 128).transpose(2, 1, 0, 3)).reshape(16, 128, 1024)
    w["a_wout"] = np.ascontiguousarray(np.asarray(inp["a_w_out"], f)[0].reshape(16, 128, D).transpose(1, 0, 2)).reshape(128, 16 * D)
    wkv = np.asarray(inp["w_kv"], f).reshape(8, 128, 512)
    w["w_kv_k"] = np.ascontiguousarray(wkv[:, :, :256].transpose(1, 0, 2)).reshape(128, 2048)
    w["w_kv_v"] = np.ascontiguousarray(wkv[:, :, 256:].transpose(1, 0, 2)).reshape(128, 2048)
    return w


def prep_l1(inp):
    f = np.float32
    w = {}
    w["w_q"] = np.ascontiguousarray(np.asarray(inp["b_w_q"], f)[0].reshape(8, 128, D).transpose(1, 0, 2)).reshape(128, 8 * D)
    w["w_o"] = np.ascontiguousarray(np.asarray(inp["b_w_o"], f)[0].reshape(8, 128, D).transpose(1, 0, 2)).reshape(128, 8 * D)
    return w


def core_masks(h):
    s = np.arange(128)[:, None]
    t = np.arange(128)[None, :]
    diag = np.tile((s < t).astype(np.float32), (1, 4))
    ones = np.ones((128, 512), np.float32)
    zeros = np.zeros((128, 512), np.float32)
    if h == 0:
        m = [zeros, diag, diag, ones]
    else:
        m = [diag, ones, zeros, diag]
    return np.ascontiguousarray(np.concatenate(m, axis=1))


def shard_x(x):
    x = np.asarray(x, np.float32)
    out = []
    for c in range(8):
        b, h = divmod(c, 2)
        rows = np.concatenate([np.arange(local_to_global(h, j) * 128, local_to_global(h, j) * 128 + 128)
                               for j in range(NCH)])
        out.append(np.ascontiguousarray(x[b][rows]))
    return out


def unshard(ys):
    out = np.zeros((4, S, D), np.float32)
    for c in range(8):
        b, h = divmod(c, 2)
        for j in range(NCH):
            g = local_to_global(h, j)
            out[b, g * 128:(g + 1) * 128] = ys[c][j * 128:(j + 1) * 128]
    return out


def kernel(**inp):
    w = dict(prep_weights(inp), **prep_l0(inp))
    w.update(prep_l1(inp))
    xs = shard_x(inp["x"])
    nc = build()
    in_maps = [dict(w, x=xs[c], masks=core_masks(c % 2)) for c in range(8)]
    res = run_bass_kernel_spmd(nc, in_maps, core_ids=list(range(8))).results
    return unshard([r["y"] for r in res])
```

```python
import contextlib
import numpy as np
import concourse.bass as bass
import concourse.mybir as mybir
from concourse.bass_utils import run_bass_kernel_spmd

F32 = mybir.dt.float32
BF16 = mybir.dt.bfloat16
AF = mybir.ActivationFunctionType
ALU = mybir.AluOpType

D = 1024
DFF = 2816
NFC = 22
S = 4096
NCH = 16
TOK = NCH * 128
NORM_EPS = 1e-6
LN_EPS = 1e-5


def chunk_owner(gb):
    m, r = divmod(gb, 4)
    if r == 0:
        return 0, 2 * m
    if r == 3:
        return 0, 2 * m + 1
    if r == 1:
        return 1, 2 * m
    return 1, 2 * m + 1


def local_to_global(h, j):
    m, r = divmod(j, 2)
    if h == 0:
        return 4 * m + (0 if r == 0 else 3)
    return 4 * m + (1 if r == 0 else 2)


class Buf:
    __slots__ = ("name", "w", "r", "dsem")

    def __init__(self, name):
        self.name = name
        self.w = None
        self.r = {}
        self.dsem = None


class EngS:
    def __init__(self, key, eng, sem):
        self.key = key
        self.eng = eng
        self.sem = sem
        self.count = 0
        self.seen = {}


class Tracker:
    def __init__(self, nc, es, n_dma_sems=40):
        self.nc = nc
        self.E = {}
        for key, eng in (("pe", nc.tensor), ("act", nc.scalar), ("dve", nc.vector),
                         ("pool", nc.gpsimd), ("sp", nc.sync)):
            self.E[key] = EngS(key, eng, es.enter_context(nc.semaphore("s_" + key)))
        self.dpool = [["d%d" % i, es.enter_context(nc.semaphore("s_d%d" % i)), 0] for i in range(n_dma_sems)]
        self.dfree = {"pool": list(range(0, 18)), "sp": list(range(18, 38)), "cc": list(range(38, n_dma_sems))}
        self.downer = {}
        self.dused = {}
        self.bufs = []

    def buf(self, name):
        b = Buf(name)
        self.bufs.append(b)
        return b

    def bufs_n(self, name, n):
        return [self.buf("%s%d" % (name, i)) for i in range(n)]

    def _deps(self, reads, writes):
        deps = {}

        def add(tok):
            if tok is None:
                return
            k, s, v = tok
            if k not in deps or deps[k][1] < v:
                deps[k] = (s, v)
        for b in reads:
            add(b.w)
        for b in writes:
            add(b.w)
            for t in b.r.values():
                add(t)
        return deps

    def _wait(self, e, deps):
        for k, (s, v) in deps.items():
            if e.key == "pe" and k == "pe":
                continue
            if e.seen.get(k, 0) >= v:
                continue
            e.eng.wait_ge(s, v)
            e.seen[k] = v

    def _mark(self, tok, reads, writes):
        k = tok[0]
        for b in reads:
            b.r[k] = tok
        for b in writes:
            b.w = tok
            b.r = {}

    def op(self, en, fn, reads=(), writes=()):
        e = self.E[en]
        self._wait(e, self._deps(reads, writes))
        inst = fn(e.eng)
        e.count += 1
        inst.then_inc(e.sem, 1)
        tok = (e.key, e.sem, e.count)
        self._mark(tok, reads, writes)
        return tok

    def dma(self, en, out, in_, reads=(), writes=(), sembuf=None, fn=None, inc=16):
        e = self.E[en]
        self._wait(e, self._deps(reads, writes))
        sb = sembuf if sembuf is not None else (writes[0] if writes else reads[0])
        if sb.dsem is None:
            cls = "cc" if fn is not None else en
            sb.dsem = self.dfree[cls].pop()
            self.downer[sb.dsem] = cls
            self.dused[id(sb)] = sb
        ent = self.dpool[sb.dsem]
        inst = fn(e.eng) if fn is not None else e.eng.dma_start(out=out, in_=in_)
        ent[2] += inc
        inst.then_inc(ent[1], inc)
        tok = (ent[0], ent[1], ent[2])
        self._mark(tok, reads, writes)
        return tok

    def barrier(self):
        deps = {}
        for e in self.E.values():
            if e.count:
                deps[e.key] = (e.sem, e.count)
        for ent in self.dpool:
            if ent[2]:
                deps[ent[0]] = (ent[1], ent[2])
        for e in self.E.values():
            d = {k: v for k, v in deps.items() if k != e.key}
            self._wait(e, d)
        for b in self.bufs:
            b.w = None
            b.r = {}
            if b.dsem is not None:
                self.dfree[self.downer[b.dsem]].append(b.dsem)
                b.dsem = None
        self.dused = {}
        self.bufs = []


class Arena:
    def __init__(self, ap_f32, total_f32):
        self.ap = ap_f32
        self.total = total_f32
        self.off = 0
        self.marks = []

    def f32(self, n):
        assert self.off + n <= self.total, ("SBUF arena overflow", self.off, n, self.total)
        v = self.ap[:, self.off:self.off + n]
        self.off += n
        return v

    def bf16(self, n):
        assert n % 2 == 0
        return self.f32(n // 2).bitcast(BF16)

    def mark(self):
        self.marks.append(self.off)

    def release(self):
        self.off = self.marks.pop()


class Ring:
    def __init__(self, T, name, views):
        self.views = views
        self.bufs = T.bufs_n(name, len(views))
        self.i = 0

    def next(self):
        k = self.i % len(self.views)
        self.i += 1
        return self.views[k], self.bufs[k]


class Prog:
    def __init__(self, mode):
        self.mode = mode
        self.nc = bass.Bass("TRN2", target_bir_lowering=False, num_devices=8)
        self.dr = {}

    def din(self, name, shape, dt=F32):
        self.dr[name] = self.nc.dram_tensor(name, list(shape), dt, kind="ExternalInput").ap()
        return self.dr[name]

    def dout(self, name, shape, dt=F32):
        self.dr[name] = self.nc.dram_tensor(name, list(shape), dt, kind="ExternalOutput").ap()
        return self.dr[name]

    def setup(self, es):
        nc = self.nc
        self.T = Tracker(nc, es)
        ARENA_F32 = 53200
        arena_t = es.enter_context(nc.sbuf_tensor("arena", [128, ARENA_F32], F32))
        self.A = Arena(arena_t[:], ARENA_F32)
        ps_t = es.enter_context(nc.psum_tensor("ps", [128, 8 * 512], F32))
        self.ps = ps_t[:]
        self.psb = self.T.bufs_n("psb", 8)
        A, T = self.A, self.T
        self.X = A.f32(NCH * D).rearrange("p (m d) -> p m d", d=D)
        self.Xb = T.bufs_n("X", NCH)
        self.ident = A.bf16(128)
        self.identb = T.buf("ident")
        self.ones_bf = A.bf16(128)
        self.onesb = T.buf("ones")
        self.gring_v = [A.f32(D) for _ in range(2)]
        self.xs_v = [A.bf16(D) for _ in range(2)]
        self.junk = A.bf16(D)
        self.stat = A.f32(64)
        tmpf = A.f32(128)
        tmpb = T.buf("tmpf")
        self.persist = [self.identb, self.onesb] + self.Xb + self.psb
        T.op("pool", lambda e: e.memset(tmpf, 1.0), writes=[tmpb])
        T.op("pool", lambda e: e.affine_select(out=tmpf, in_=tmpf, pattern=[[-1, 128]], compare_op=ALU.is_equal,
                                               fill=0.0, base=0, channel_multiplier=1), reads=[tmpb], writes=[tmpb])
        T.op("dve", lambda e: e.tensor_copy(out=self.ident, in_=tmpf), reads=[tmpb], writes=[self.identb])
        T.op("pool", lambda e: e.memset(self.ones_bf, 1.0), writes=[self.onesb])
        self.phase_barrier()

    def phase_barrier(self):
        T = self.T
        T.barrier()
        T.bufs = []
        self.gring = Ring(T, "gring", self.gring_v)
        self.xsring = Ring(T, "xs", self.xs_v)
        self.junkb = T.buf("junk")
        self.statb = T.buf("stat")
        for b in self.persist:
            b.w = None
            b.r = {}

    def bank(self, i, n=1):
        return self.ps[:, i * 512:(i + n) * 512]

    def load_x(self, x_dram):
        T = self.T
        xv = x_dram.rearrange("(m p) d -> p m d", p=128)
        for q in range(4):
            T.dma("sp", self.X[:, q * 4:(q + 1) * 4, :], xv[:, q * 4:(q + 1) * 4, :],
                  writes=self.Xb[q * 4:(q + 1) * 4])

    def rms_stats(self, chunks, col0, scr0=32):
        T = self.T
        n = len(chunks)
        ss = self.stat[:, scr0:scr0 + n]
        T.op("dve", lambda e: e.memset(self.stat[:, scr0:scr0 + 2 * n], 0.0), writes=[self.statb])
        for i, m in enumerate(chunks):
            T.op("act", lambda e, i=i, m=m: e.activation(out=self.junk, in_=self.X[:, m, :], func=AF.Square,
                                                         accum_out=self.stat[:, scr0 + i:scr0 + i + 1]),
                 reads=[self.Xb[m]], writes=[self.junkb, self.statb])
        ms = self.stat[:, scr0 + n:scr0 + 2 * n]
        T.op("dve", lambda e: e.tensor_scalar(out=ms, in0=ss, scalar1=1.0 / D, scalar2=NORM_EPS,
                                              op0=ALU.mult, op1=ALU.add), reads=[self.statb], writes=[self.statb])
        T.op("act", lambda e: e.activation(out=ms, in_=ms, func=AF.Ln), reads=[self.statb], writes=[self.statb])
        T.op("act", lambda e: e.activation(out=self.stat[:, col0:col0 + n], in_=ms, func=AF.Exp, scale=-0.5),
             reads=[self.statb], writes=[self.statb])

    def load_gain(self, row):
        T = self.T
        gv, gb = self.gring.next()
        T.dma("sp", gv, self.dr["gains"][row:row + 1, :].broadcast_to([128, D]), writes=[gb])
        return gv, gb

    def norm_to_ht(self, chunks, gain_row, HT, HTb, tok0, trbanks, gain=None, col0=0, scr0=32):
        T = self.T
        gv, gb = gain if gain is not None else self.load_gain(gain_row)
        self.rms_stats(chunks, col0, scr0)
        for i, m in enumerate(chunks):
            xs, xsb = self.xsring.next()
            T.op("dve", lambda e, i=i, m=m, xs=xs: e.scalar_tensor_tensor(
                out=xs, in0=self.X[:, m, :], scalar=self.stat[:, col0 + i:col0 + i + 1], in1=gv,
                op0=ALU.mult, op1=ALU.mult),
                reads=[self.Xb[m], self.statb, gb], writes=[xsb])
            bk = trbanks[i % len(trbanks)]
            trv = self.bank(bk).bitcast(BF16)
            for kc in range(8):
                T.op("pe", lambda e, kc=kc, xs=xs, trv=trv: e.transpose(
                    trv[:, kc * 128:(kc + 1) * 128], xs[:, kc * 128:(kc + 1) * 128], self.ident),
                    reads=[xsb, self.identb], writes=[self.psb[bk]])
            dst = HT[:, :, tok0 + i * 128: tok0 + (i + 1) * 128]
            src = trv.rearrange("p (k t) -> p k t", t=128)
            if i % 2 == 0:
                T.op("act", lambda e, dst=dst, src=src: e.copy(out=dst, in_=src), reads=[self.psb[bk]], writes=[HTb[i]])
            else:
                T.op("dve", lambda e, dst=dst, src=src: e.tensor_copy(out=dst, in_=src), reads=[self.psb[bk]], writes=[HTb[i]])

    def norm_part_a(self, chunks, gain, col0, scr0, ring):
        T = self.T
        gv, gb = gain
        self.rms_stats(chunks, col0, scr0)
        out = []
        for i, m in enumerate(chunks):
            xs, xsb = ring.next()
            T.op("dve", lambda e, i=i, m=m, xs=xs: e.scalar_tensor_tensor(
                out=xs, in0=self.X[:, m, :], scalar=self.stat[:, col0 + i:col0 + i + 1], in1=gv,
                op0=ALU.mult, op1=ALU.mult),
                reads=[self.Xb[m], self.statb, gb], writes=[xsb])
            out.append((xs, xsb))
        return out

    def norm_part_b(self, xss, HT, HTb, tok0, trbanks):
        T = self.T
        for i, (xs, xsb) in enumerate(xss):
            bk = trbanks[i % len(trbanks)]
            trv = self.bank(bk).bitcast(BF16)
            for kc in range(8):
                T.op("pe", lambda e, kc=kc, xs=xs, trv=trv: e.transpose(
                    trv[:, kc * 128:(kc + 1) * 128], xs[:, kc * 128:(kc + 1) * 128], self.ident),
                    reads=[xsb, self.identb], writes=[self.psb[bk]])
            dst = HT[:, :, tok0 + i * 128: tok0 + (i + 1) * 128]
            src = trv.rearrange("p (k t) -> p k t", t=128)
            if i % 2 == 0:
                T.op("act", lambda e, dst=dst, src=src: e.copy(out=dst, in_=src), reads=[self.psb[bk]], writes=[HTb[i]])
            else:
                T.op("dve", lambda e, dst=dst, src=src: e.tensor_copy(out=dst, in_=src), reads=[self.psb[bk]], writes=[HTb[i]])

    def ffn(self, f_idx, gain_row, final=None, gain=None):
        T, A = self.T, self.A
        A.mark()
        G = 6
        HT = A.bf16(8 * TOK).rearrange("p (k t) -> p k t", t=TOK)
        HTb = T.bufs_n("HT", NCH)
        ACTT = A.bf16(G * TOK).rearrange("p (g t) -> p g t", t=TOK)
        ACTb = [[T.buf("actt%d_%d" % (g, ts)) for ts in range(4)] for g in range(G)]
        WOUT = A.bf16(NFC * D).rearrange("p (i n) -> p i n", n=D)
        WOUTb = T.buf("wout")
        winr = Ring(T, "win", [A.bf16(2048) for _ in range(3)])
        sgr = Ring(T, "sg", [A.f32(512) for _ in range(2)])
        win_d = self.dr["ffn_win"]
        wout_d = self.dr["ffn_wout"]
        wq_bounds = [0, 6, 12, 17, 22]
        wout_loaded = [0]

        def load_wout_piece():
            q = wout_loaded[0]
            if q >= 4:
                return
            wout_loaded[0] += 1
            a, b = wq_bounds[q], wq_bounds[q + 1]
            T.dma("pool", WOUT[:, a:b, :], wout_d[f_idx, :, a * D:b * D].rearrange("p (i n) -> p i n", n=D),
                  writes=[WOUTb])
        loaded = []

        def ensure(i):
            while len(loaded) <= min(i + 2, NFC - 1):
                v, b = winr.next()
                T.dma("pool", v, win_d[f_idx, len(loaded), :, :], writes=[b])
                loaded.append((v, b))
        ensure(0)
        gain = gain if gain is not None else self.load_gain(gain_row)
        normed = set()

        xs4 = Ring(T, "xs4", self.xs_v + [A.bf16(D) for _ in range(2)])
        parts = {}

        def part_a(ts):
            parts[ts] = self.norm_part_a([4 * ts + c for c in range(4)], gain, 4 * ts, 32 + 8 * ts, xs4)

        def part_b(ts):
            normed.add(ts)
            self.norm_part_b(parts[ts], HT, HTb[4 * ts:4 * ts + 4], ts * 512, [6, 7])

        def ensure_ht(ts):
            assert ts in normed
        part_a(0)
        part_b(0)
        part_a(1)
        fin_gain = self.load_gain(final[0]) if final is not None else None
        fin_buf = T.buf("finstore") if final is not None else None
        unit = 0
        groups = [list(range(g0, min(g0 + G, NFC))) for g0 in range(0, NFC, G)]
        ob = 0
        ucount = [0]

        def unit_fn(il, i, ts):
            wv, wb = loaded[i]
            ensure_ht(ts)
            bg, bu = (0, 1) if ucount[0] % 2 == 0 else (2, 3)
            ucount[0] += 1
            rhs_b = HTb[ts * 4:(ts + 1) * 4]
            for which, bk in ((0, bg), (1, bu)):
                for kc in range(8):
                    T.op("pe", lambda e, kc=kc, which=which, bk=bk: e.matmul(
                        self.bank(bk), lhsT=wv[:, kc * 256 + which * 128: kc * 256 + which * 128 + 128],
                        rhs=HT[:, kc, ts * 512:(ts + 1) * 512], start=(kc == 0), stop=(kc == 7)),
                        reads=[wb] + rhs_b, writes=[self.psb[bk]])
            sg, sgb = sgr.next()
            T.op("act", lambda e: e.activation(out=sg, in_=self.bank(bg), func=AF.Silu),
                 reads=[self.psb[bg]], writes=[sgb])
            T.op("dve", lambda e: e.tensor_tensor(
                out=ACTT[:, il, ts * 512:(ts + 1) * 512], in0=sg, in1=self.bank(bu), op=ALU.mult),
                reads=[sgb, self.psb[bu]], writes=[ACTb[il][ts]])

        for gi, grp in enumerate(groups):
            if gi == 0:
                ensure(0)
                for _ in range(3):
                    load_wout_piece()
                for ts in range(4):
                    for il in range(3):
                        unit_fn(il, grp[il], ts)
                    if ts + 1 < 4:
                        part_b(ts + 1)
                    if ts + 2 < 4:
                        part_a(ts + 2)
                rest = list(enumerate(grp))[3:]
            else:
                rest = list(enumerate(grp))
            for il, i in rest:
                ensure(i)
                load_wout_piece()
                for ts in range(4):
                    unit_fn(il, i, ts)
            for m in range(NCH):
                b0 = 4 if ob % 2 == 0 else 6
                ob += 1
                for n in range(2):
                    for il, i in enumerate(grp):
                        T.op("pe", lambda e, il=il, i=i, n=n, m=m, b0=b0: e.matmul(
                            self.bank(b0 + n), lhsT=ACTT[:, il, m * 128:(m + 1) * 128],
                            rhs=WOUT[:, i, n * 512:(n + 1) * 512], start=(il == 0), stop=(il == len(grp) - 1)),
                            reads=[ACTb[il][m // 4], WOUTb], writes=[self.psb[b0 + n]])
                T.op("dve", lambda e, m=m, b0=b0: e.scalar_tensor_tensor(
                    out=self.X[:, m, :], in0=self.bank(b0, 2), scalar=0.5, in1=self.X[:, m, :],
                    op0=ALU.mult, op1=ALU.add),
                    reads=[self.psb[b0], self.psb[b0 + 1], self.Xb[m]], writes=[self.Xb[m]])
                if final is not None and grp is groups[-1] and m % 4 == 3:
                    q = m // 4
                    cks = [4 * q + c for c in range(4)]
                    self.rms_stats(cks, 16 + 4 * q, 32 + 8 * q)
                    gv, gb = fin_gain
                    for mm in cks:
                        T.op("dve", lambda e, mm=mm, q=q: e.scalar_tensor_tensor(
                            out=self.X[:, mm, :], in0=self.X[:, mm, :],
                            scalar=self.stat[:, 16 + mm:17 + mm], in1=gv, op0=ALU.mult, op1=ALU.mult),
                            reads=[self.Xb[mm], self.statb, gb], writes=[self.Xb[mm]])
                    ov = final[1].rearrange("(m p) d -> p m d", p=128)
                    T.dma("sp", ov[:, q * 4:(q + 1) * 4, :], self.X[:, q * 4:(q + 1) * 4, :],
                          reads=self.Xb[q * 4:(q + 1) * 4], sembuf=fin_buf)
        self.phase_barrier()
        A.release()

    def store_x(self, out_dram, gain_row=None):
        T, A = self.T, self.A
        A.mark()
        ov = out_dram.rearrange("(m p) d -> p m d", p=128)
        stb = T.buf("store")
        if gain_row is None:
            for q in range(4):
                T.dma("sp", ov[:, q * 4:(q + 1) * 4, :], self.X[:, q * 4:(q + 1) * 4, :],
                      reads=self.Xb[q * 4:(q + 1) * 4], sembuf=stb)
        else:
            gv, gb = self.load_gain(gain_row)
            self.rms_stats(list(range(NCH)), 0)
            for m in range(NCH):
                T.op("dve", lambda e, m=m: e.scalar_tensor_tensor(
                    out=self.X[:, m, :], in0=self.X[:, m, :], scalar=self.stat[:, m:m + 1], in1=gv,
                    op0=ALU.mult, op1=ALU.mult), reads=[self.Xb[m], self.statb, gb], writes=[self.Xb[m]])
                if m % 4 == 3:
                    q = m // 4
                    T.dma("sp", ov[:, q * 4:(q + 1) * 4, :], self.X[:, q * 4:(q + 1) * 4, :],
                          reads=self.Xb[q * 4:(q + 1) * 4], sembuf=stb)
        self.phase_barrier()
        A.release()

    def sgu(self):
        T, A, d = self.T, self.A, self.dr
        A.mark()
        AX = mybir.AxisListType
        GELU = AF.Gelu_apprx_tanh
        wsT_f = A.f32(1024)
        wsTb = T.buf("wsT")
        T.dma("sp", wsT_f, d["a_wsT"][:, :], writes=[wsTb])
        w3 = wsT_f.rearrange("p (g t) -> p g t", t=128)
        T.op("pool", lambda e: e.affine_select(out=w3, in_=w3, pattern=[[0, 8], [1, 128]], compare_op=ALU.is_ge,
                                               fill=0.0, base=0, channel_multiplier=-1), reads=[wsTb], writes=[wsTb])
        wsT_bf = A.bf16(1024)
        wsbfb = T.buf("wsbf")
        T.op("dve", lambda e: e.tensor_copy(out=wsT_bf, in_=wsT_f), reads=[wsTb], writes=[wsbfb])
        BS = A.f32(1024)
        BSb = T.buf("bs")
        T.dma("sp", BS, d["a_bs"][0:1, :].broadcast_to([128, 1024]), writes=[BSb])
        lng = A.f32(16)
        lnb = A.f32(16)
        lnbuf = T.buf("ln")
        T.dma("sp", lng, d["a_lng"][:, :], writes=[lnbuf])
        T.dma("sp", lnb, d["a_lnb"][:, :], writes=[lnbuf])
        T1 = A.f32(16 * 128).rearrange("p (f t) -> p f t", t=128)
        T1b = T.buf("T1")
        for g in range(8):
            T.op("pe", lambda e, g=g: e.matmul(self.ps[:, g * 128:(g + 1) * 128], lhsT=self.ones_bf,
                                                rhs=wsT_bf[:, g * 128:(g + 1) * 128], start=True, stop=True),
                 reads=[self.onesb, wsbfb], writes=[self.psb[g // 4]])
        for fc in range(16):
            g = fc // 2
            T.op("dve", lambda e, fc=fc, g=g: e.scalar_tensor_tensor(
                out=T1[:, fc, :], in0=self.ps[:, g * 128:(g + 1) * 128], scalar=lnb[:, fc:fc + 1],
                in1=BS[:, g * 128:(g + 1) * 128], op0=ALU.mult, op1=ALU.add),
                reads=[self.psb[g // 4], lnbuf, BSb], writes=[T1b])
        HTt = A.bf16(8 * 512).rearrange("p (k t) -> p k t", t=512)
        HTtb = T.bufs_n("HTt", 4)
        VRAW = A.bf16(4 * 2048).rearrange("p (c f) -> p c f", f=2048)
        VRAWb = T.bufs_n("vraw", 4)
        vtr = Ring(T, "vt", [A.f32(256) for _ in range(3)])
        UVT = A.bf16(16 * 512).rearrange("p (f t) -> p f t", t=512)
        UVTb = T.bufs_n("uvt", 16)
        awvr = Ring(T, "awv", [A.bf16(2048) for _ in range(4)])
        awur = Ring(T, "awu", [A.bf16(1024) for _ in range(5)])
        awor = Ring(T, "awo", [A.bf16(1024) for _ in range(8)])
        usbr = Ring(T, "usb", [A.f32(512) for _ in range(2)])
        tmpr = Ring(T, "tmp", [A.f32(512) for _ in range(2)])
        wsTp = [A.bf16(1024) for _ in range(4)]
        wsTpb = T.bufs_n("wsTp", 4)
        NEGMU = [A.bf16(128) for _ in range(4)]
        NEGMUb = T.bufs_n("negmu", 4)
        st = A.f32(128)
        stb = T.buf("sgst")
        SV = st[:, 0:32].rearrange("p (c f) -> p c f", f=8)
        SQ = st[:, 32:64].rearrange("p (c f) -> p c f", f=8)
        S1, S2, MEAN, MSQ, VAR, RSTD, NMU = (st[:, 64 + 4 * i:68 + 4 * i] for i in range(7))
        vb = 0
        gain1 = self.load_gain(1)
        xs4 = Ring(T, "xs4", self.xs_v + [A.bf16(D) for _ in range(2)])

        def sgu_a(tt):
            return self.norm_part_a([4 * tt + c for c in range(4)], gain1, 4 * tt, 32 + 8 * tt, xs4)
        self.norm_part_b(sgu_a(0), HTt, HTtb, 0, [6, 7])
        for tt in range(4):
            chunks = [4 * tt + c for c in range(4)]
            T.op("dve", lambda e: e.memset(st[:, 0:64], 0.0), writes=[stb])
            for ft in range(8):
                av, avb = awvr.next()
                T.dma("pool", av, d["a_win_v"][ft, :, :], writes=[avb])
                for c in range(4):
                    bk = vb % 4
                    vb += 1
                    pv = self.ps[:, bk * 512: bk * 512 + 256]
                    for kc in range(8):
                        T.op("pe", lambda e, kc=kc, c=c, av=av, pv=pv: e.matmul(
                            pv, lhsT=HTt[:, kc, c * 128:(c + 1) * 128], rhs=av[:, kc * 256:(kc + 1) * 256],
                            start=(kc == 0), stop=(kc == 7)), reads=[HTtb[c], avb], writes=[self.psb[bk]])
                    vt, vtb = vtr.next()
                    T.op("act", lambda e, vt=vt, pv=pv, c=c, ft=ft: e.activation(
                        out=vt, in_=pv, func=GELU, accum_out=SV[:, c, ft:ft + 1]),
                        reads=[self.psb[bk]], writes=[vtb, stb])
                    T.op("act", lambda e, vt=vt, c=c, ft=ft: e.activation(
                        out=self.junk[:, 0:256], in_=vt, func=AF.Square, accum_out=SQ[:, c, ft:ft + 1]),
                        reads=[vtb], writes=[self.junkb, stb])
                    T.op("dve", lambda e, vt=vt, c=c, ft=ft: e.tensor_copy(
                        out=VRAW[:, c, ft * 256:(ft + 1) * 256], in_=vt), reads=[vtb], writes=[VRAWb[c]])
            T.op("dve", lambda e: e.tensor_reduce(out=S1, in_=SV, axis=AX.X, op=ALU.add), reads=[stb], writes=[stb])
            T.op("dve", lambda e: e.tensor_reduce(out=S2, in_=SQ, axis=AX.X, op=ALU.add), reads=[stb], writes=[stb])
            T.op("dve", lambda e: e.tensor_scalar(out=MEAN, in0=S1, scalar1=1.0 / 2048, scalar2=None, op0=ALU.mult),
                 reads=[stb], writes=[stb])
            T.op("dve", lambda e: e.tensor_tensor(out=MSQ, in0=MEAN, in1=MEAN, op=ALU.mult), reads=[stb], writes=[stb])
            T.op("dve", lambda e: e.scalar_tensor_tensor(out=VAR, in0=S2, scalar=1.0 / 2048, in1=MSQ,
                                                         op0=ALU.mult, op1=ALU.subtract), reads=[stb], writes=[stb])
            T.op("dve", lambda e: e.tensor_scalar(out=VAR, in0=VAR, scalar1=LN_EPS, scalar2=None, op0=ALU.add),
                 reads=[stb], writes=[stb])
            T.op("act", lambda e: e.activation(out=VAR, in_=VAR, func=AF.Ln), reads=[stb], writes=[stb])
            T.op("act", lambda e: e.activation(out=RSTD, in_=VAR, func=AF.Exp, scale=-0.5), reads=[stb], writes=[stb])
            T.op("dve", lambda e: e.tensor_scalar(out=NMU, in0=MEAN, scalar1=-1.0, scalar2=None, op0=ALU.mult),
                 reads=[stb], writes=[stb])
            for c in range(4):
                T.op("dve", lambda e, c=c: e.tensor_scalar(out=wsTp[c], in0=wsT_f, scalar1=RSTD[:, c:c + 1],
                                                           scalar2=None, op0=ALU.mult),
                     reads=[wsTb, stb], writes=[wsTpb[c]])
                T.op("dve", lambda e, c=c: e.tensor_scalar(out=NEGMU[c], in0=self.ones_bf, scalar1=NMU[:, c:c + 1],
                                                           scalar2=None, op0=ALU.mult),
                     reads=[self.onesb, stb], writes=[NEGMUb[c]])
            aw_first = []
            for fcl in range(8):
                v_, b_ = awor.next()
                T.dma("pool", v_, d["a_wout"][:, fcl * D:(fcl + 1) * D], writes=[b_])
                aw_first.append((v_, b_))
            for fc in range(16):
                g = fc // 2
                au, aub = awur.next()
                T.dma("pool", au, d["a_win_u"][fc, :, :], writes=[aub])
                bu = 4 + fc % 2
                bs_ = 6 + fc % 2
                for kc in range(8):
                    T.op("pe", lambda e, kc=kc, au=au, bu=bu: e.matmul(
                        self.bank(bu), lhsT=au[:, kc * 128:(kc + 1) * 128], rhs=HTt[:, kc, :],
                        start=(kc == 0), stop=(kc == 7)), reads=[aub] + HTtb, writes=[self.psb[bu]])
                usb, usbb = usbr.next()
                T.op("act", lambda e, usb=usb, bu=bu: e.activation(out=usb, in_=self.bank(bu), func=GELU),
                     reads=[self.psb[bu]], writes=[usbb])
                for c in range(4):
                    po = self.ps[:, bs_ * 512 + c * 128: bs_ * 512 + (c + 1) * 128]
                    T.op("pe", lambda e, c=c, fc=fc, g=g, po=po: e.matmul(
                        po, lhsT=VRAW[:, c, fc * 128:(fc + 1) * 128], rhs=wsTp[c][:, g * 128:(g + 1) * 128],
                        start=True, stop=False), reads=[VRAWb[c], wsTpb[c]], writes=[self.psb[bs_]])
                    T.op("pe", lambda e, c=c, g=g, po=po: e.matmul(
                        po, lhsT=NEGMU[c], rhs=wsTp[c][:, g * 128:(g + 1) * 128], start=False, stop=True),
                        reads=[NEGMUb[c], wsTpb[c]], writes=[self.psb[bs_]])
                tmp, tmpb = tmpr.next()
                T.op("dve", lambda e, tmp=tmp, fc=fc, bs_=bs_: e.scalar_tensor_tensor(
                    out=tmp.rearrange("p (c t) -> p c t", t=128),
                    in0=self.bank(bs_).rearrange("p (c t) -> p c t", t=128), scalar=lng[:, fc:fc + 1],
                    in1=T1[:, fc:fc + 1, :].broadcast_to([128, 4, 128]), op0=ALU.mult, op1=ALU.add),
                    reads=[self.psb[bs_], lnbuf, T1b], writes=[tmpb])
                T.op("dve", lambda e, tmp=tmp, usb=usb, fc=fc: e.tensor_tensor(
                    out=UVT[:, fc, :], in0=tmp, in1=usb, op=ALU.mult), reads=[tmpb, usbb], writes=[UVTb[fc]])
            nxt = sgu_a(tt + 1) if tt + 1 < 4 else None
            ob = 0
            for hf in range(2):
                aw = aw_first if hf == 0 else []
                for fcl in range(8 if hf == 1 else 0):
                    v_, b_ = awor.next()
                    fc = hf * 8 + fcl
                    T.dma("pool", v_, d["a_wout"][:, fc * D:(fc + 1) * D], writes=[b_])
                    aw.append((v_, b_))
                for c in range(4):
                    b0 = 0 if ob % 2 == 0 else 2
                    ob += 1
                    for n in range(2):
                        for fcl in range(8):
                            fc = hf * 8 + fcl
                            T.op("pe", lambda e, fc=fc, fcl=fcl, c=c, n=n, b0=b0: e.matmul(
                                self.bank(b0 + n), lhsT=UVT[:, fc, c * 128:(c + 1) * 128],
                                rhs=aw[fcl][0][:, n * 512:(n + 1) * 512], start=(fcl == 0), stop=(fcl == 7)),
                                reads=[UVTb[fc], aw[fcl][1]], writes=[self.psb[b0 + n]])
                    m = chunks[c]
                    T.op("dve", lambda e, m=m, b0=b0: e.tensor_tensor(
                        out=self.X[:, m, :], in0=self.bank(b0, 2), in1=self.X[:, m, :], op=ALU.add),
                        reads=[self.psb[b0], self.psb[b0 + 1], self.Xb[m]], writes=[self.Xb[m]])
            if nxt is not None:
                self.norm_part_b(nxt, HTt, HTtb, 0, [6, 7])
        self.phase_barrier()
        A.release()

    def kv_exchange(self):
        T, A, d = self.T, self.A, self.dr
        nc = self.nc
        A.mark()
        HT = A.bf16(8 * TOK).rearrange("p (k t) -> p k t", t=TOK)
        HTb = T.bufs_n("HT", NCH)
        wk = A.bf16(2048)
        wv = A.bf16(2048)
        wkb, wvb = T.buf("wk"), T.buf("wv")
        T.dma("pool", wk, d["w_kv_k"][:, :], writes=[wkb])
        T.dma("pool", wv, d["w_kv_v"][:, :], writes=[wvb])
        KVS = A.bf16(8192)
        KVSb = T.buf("kvs")
        gain3 = self.load_gain(3)
        u = 0
        xs4 = Ring(T, "xs4", self.xs_v + [A.bf16(D) for _ in range(2)])
        kparts = {}

        def kv_a(ts):
            kparts[ts] = self.norm_part_a([4 * ts + c for c in range(4)], gain3, 4 * ts, 32 + 8 * ts, xs4)

        def kv_b(ts):
            self.norm_part_b(kparts[ts], HT, HTb[4 * ts:4 * ts + 4], ts * 512, [6, 7])
        kv_a(0)
        kv_b(0)
        kv_a(1)
        for ts in range(4):
            if ts > 0:
                pass
            for jj in range(2):
                bk = u % 4
                u += 1
                for kc in range(8):
                    T.op("pe", lambda e, kc=kc, jj=jj, ts=ts, bk=bk: e.matmul(
                        self.bank(bk), lhsT=wk[:, kc * 256 + jj * 128: kc * 256 + jj * 128 + 128],
                        rhs=HT[:, kc, ts * 512:(ts + 1) * 512], start=(kc == 0), stop=(kc == 7)),
                        reads=[wkb] + HTb[ts * 4:(ts + 1) * 4], writes=[self.psb[bk]])
                dst = KVS[:, jj * TOK + ts * 512: jj * TOK + (ts + 1) * 512]
                if u % 2 == 0:
                    T.op("dve", lambda e, dst=dst, bk=bk: e.tensor_copy(out=dst, in_=self.bank(bk)),
                         reads=[self.psb[bk]], writes=[KVSb])
                else:
                    T.op("act", lambda e, dst=dst, bk=bk: e.copy(out=dst, in_=self.bank(bk)),
                         reads=[self.psb[bk]], writes=[KVSb])
            for m in range(4 * ts, 4 * ts + 4):
                bk = u % 4
                u += 1
                pv = self.ps[:, bk * 512: bk * 512 + 256]
                for kc in range(8):
                    T.op("pe", lambda e, kc=kc, m=m, pv=pv: e.matmul(
                        pv, lhsT=HT[:, kc, m * 128:(m + 1) * 128], rhs=wv[:, kc * 256:(kc + 1) * 256],
                        start=(kc == 0), stop=(kc == 7)), reads=[wvb, HTb[m]], writes=[self.psb[bk]])
                dst = KVS[:, 4096 + m * 256: 4096 + (m + 1) * 256]
                if u % 2 == 0:
                    T.op("dve", lambda e, dst=dst, pv=pv: e.tensor_copy(out=dst, in_=pv), reads=[self.psb[bk]], writes=[KVSb])
                else:
                    T.op("act", lambda e, dst=dst, pv=pv: e.copy(out=dst, in_=pv), reads=[self.psb[bk]], writes=[KVSb])
            if ts + 1 < 4:
                kv_b(ts + 1)
            if ts + 2 < 4:
                kv_a(ts + 2)
        bounce = nc.dram_tensor("kv_bounce", [128, 4096], F32)
        self.gathered = nc.dram_tensor("kv_gathered", [256, 4096], F32)
        bb = T.buf("bounce")
        gb = T.buf("gathered")
        T.dma("sp", bounce[:, :], KVS.bitcast(F32), reads=[KVSb], writes=[bb])
        T.dma("pool", None, None, reads=[bb], writes=[gb], inc=1,
              fn=lambda e: e.collective_compute("AllGather", ALU.bypass,
                                                replica_groups=[[0, 1], [2, 3], [4, 5], [6, 7]],
                                                ins=[bounce.ap().opt()], outs=[self.gathered.ap().opt()]))
        self.phase_barrier()
        A.release()

    def attention(self):
        T, A, d = self.T, self.A, self.dr
        A.mark()
        gain5 = self.load_gain(5)
        KVr = [A.bf16(8192) for _ in range(2)]
        KTb = T.buf("KT")
        Vb = KTb
        gat = self.gathered.ap()
        WQ = A.bf16(8 * D).rearrange("p (k n) -> p k n", n=D)
        WO = A.bf16(8 * D).rearrange("p (k n) -> p k n", n=D)
        WQb, WOb = T.buf("WQ"), T.buf("WO")
        T.dma("pool", WQ, d["w_q"][:, :].rearrange("p (k n) -> p k n", n=D), writes=[WQb])
        for h in range(2):
            T.dma("sp", KVr[h].bitcast(F32), gat[h * 128:(h + 1) * 128, :], writes=[KTb])

        def KTblk(r0, jj, kb):
            h, jl_ = chunk_owner(kb)
            return KVr[h][r0:r0 + 64, jj * TOK + jl_ * 128: jj * TOK + (jl_ + 1) * 128]

        def Vblk(kb, k):
            h, jl_ = chunk_owner(kb)
            return KVr[h][:, 4096 + jl_ * 256 + k * 64: 4096 + jl_ * 256 + (k + 1) * 64]
        MASK = A.bf16(4 * 512)
        MASKb = T.buf("mask")
        T.dma("pool", MASK, d["masks"][:, :], writes=[MASKb])
        T.dma("pool", WO, d["w_o"][:, :].rearrange("p (k n) -> p k n", n=D), writes=[WOb])
        tf = self.junk.bitcast(F32)[:, 0:128]
        tfb = self.junkb
        TRI = A.bf16(128)
        COTRI = A.bf16(128)
        TRIb, COTRIb = T.buf("tri"), T.buf("cotri")
        T.op("pool", lambda e: e.memset(tf, 1.0), writes=[tfb])
        T.op("pool", lambda e: e.affine_select(out=tf, in_=tf, pattern=[[-1, 128]], compare_op=ALU.is_ge, fill=0.0,
                                               base=0, channel_multiplier=1), reads=[tfb], writes=[tfb])
        T.op("dve", lambda e: e.tensor_copy(out=TRI, in_=tf), reads=[tfb], writes=[TRIb])
        T.op("pool", lambda e: e.memset(tf, 1.0), reads=[tfb], writes=[tfb])
        T.op("pool", lambda e: e.affine_select(out=tf, in_=tf, pattern=[[1, 128]], compare_op=ALU.is_gt, fill=0.0,
                                               base=0, channel_multiplier=-1), reads=[tfb], writes=[tfb])
        T.op("dve", lambda e: e.tensor_copy(out=COTRI, in_=tf), reads=[tfb], writes=[COTRIb])
        HTq = A.bf16(8 * 512).rearrange("p (k t) -> p k t", t=512)
        HTqb = T.bufs_n("HTq", 4)
        QT2 = [A.bf16(8 * 512).rearrange("p (h t) -> p h t", t=512) for _ in range(2)]
        QTb2 = [T.bufs_n("QT%d_" % i, 4) for i in range(2)]
        otr = Ring(T, "OT", [A.bf16(8 * 128).rearrange("p (h t) -> p h t", t=128) for _ in range(2)])
        Er = Ring(T, "E", [A.f32(1024) for _ in range(3)])
        SPr = Ring(T, "SP", [A.bf16(1024) for _ in range(4)])
        Wr = Ring(T, "W", [A.f32(1024) for _ in range(2)])
        Ar = Ring(T, "A", [A.bf16(1024) for _ in range(2)])
        zpairs = [(0, 1)]
        PB = 4
        OB = 6
        zi = [0]

        def stage_a(u):
            pp, kb, jl, j, kmax = u["pp"], u["kb"], u["jl"], u["j"], u["kmax"]
            QT, QTb = u["QT"]
            zb = zpairs[zi[0] % len(zpairs)]
            zi[0] += 1
            for c in range(2):
                k = 2 * pp + c
                r0 = c * 64
                T.op("pe", lambda e, c=c, r0=r0: e.matmul(
                    self.bank(zb[c]), lhsT=KTblk(r0, pp, kb),
                    rhs=QT[r0:r0 + 64, pp * 4:pp * 4 + 4, jl * 128:(jl + 1) * 128], start=True, stop=True),
                    reads=[KTb, QTb[k]], writes=[self.psb[zb[c]]])
            zbufs = [self.psb[zb[0]], self.psb[zb[1]]]
            E, Eb = Er.next()
            T.op("act", lambda e: e.activation(out=E, in_=self.bank(zb[0], 2), func=AF.Exp), reads=zbufs, writes=[Eb])
            SP, SPb = SPr.next()
            special = kb >= kmax - 1
            mk = None
            if special:
                mi = (j % 2) * 2 + (0 if kb == kmax else 1)
                mk = MASK[:, mi * 512:(mi + 1) * 512]
                SPf, SPfb = Wr.next()
                T.op("act", lambda e: e.activation(out=SPf, in_=E, func=AF.Ln, bias=1.0), reads=[Eb], writes=[SPfb])
                for c in range(2):
                    T.op("dve", lambda e, c=c: e.tensor_tensor(out=SP[:, c * 512:(c + 1) * 512],
                                                               in0=SPf[:, c * 512:(c + 1) * 512], in1=mk, op=ALU.mult),
                         reads=[SPfb, MASKb], writes=[SPb])
            else:
                T.op("act", lambda e: e.activation(out=SP, in_=E, func=AF.Ln, bias=1.0), reads=[Eb], writes=[SPb])
            u.update(E=E, Eb=Eb, SP=SP, SPb=SPb, mk=mk)

        def stage_b(u):
            ch = u["chain"]
            pbufs = [self.psb[PB], self.psb[PB + 1]]
            for c in range(2):
                sl = slice(c * 512, (c + 1) * 512)
                if u["first"]:
                    T.op("pe", lambda e, c=c, sl=sl: e.matmul(self.bank(PB + c), lhsT=TRI, rhs=u["SP"][:, sl],
                                                              start=True, stop=True),
                         reads=[TRIb, u["SPb"]], writes=[pbufs[c]])
                else:
                    pv = ch["prev"]
                    T.op("pe", lambda e, c=c, sl=sl, pv=pv: e.matmul(
                        self.bank(PB + c), lhsT=COTRI, rhs=pv["SP"][:, sl], start=False, stop=False,
                        skip_group_check=True), reads=[COTRIb, pv["SPb"]], writes=[pbufs[c]])
                    T.op("pe", lambda e, c=c, sl=sl: e.matmul(
                        self.bank(PB + c), lhsT=TRI, rhs=u["SP"][:, sl], start=False, stop=True,
                        skip_group_check=True), reads=[TRIb, u["SPb"]], writes=[pbufs[c]])
            ch["prev"] = u
            W, Wb = Wr.next()
            T.op("act", lambda e: e.activation(out=W, in_=self.bank(PB, 2), func=AF.Exp, scale=-1.0),
                 reads=pbufs, writes=[Wb])
            Av, Ab = Ar.next()
            T.op("dve", lambda e: e.tensor_tensor(out=Av, in0=u["E"], in1=W, op=ALU.mult),
                 reads=[u["Eb"], Wb], writes=[Ab])
            if u["mk"] is not None:
                for c in range(2):
                    T.op("dve", lambda e, c=c, Av=Av: e.tensor_tensor(out=Av[:, c * 512:(c + 1) * 512],
                                                                      in0=Av[:, c * 512:(c + 1) * 512], in1=u["mk"],
                                                                      op=ALU.mult),
                         reads=[Ab, MASKb], writes=[Ab])
            u.update(A=Av, Ab=Ab)

        def stage_c(u):
            pp, kb = u["pp"], u["kb"]
            OB = u["chain"]["ob"]
            for c in range(2):
                k = 2 * pp + c
                a3 = u["A"][:, c * 512:(c + 1) * 512].rearrange("p (h t) -> p h t", t=128)
                for hh in range(2):
                    T.op("pe", lambda e, c=c, k=k, hh=hh, a3=a3: e.matmul(
                        self.ps[hh * 64:(hh + 1) * 64, OB * 512 + c * 256: OB * 512 + (c + 1) * 256],
                        lhsT=Vblk(kb, k), rhs=a3[:, hh::2, :],
                        start=(u["first"] and c == 0), stop=u["last"], skip_group_check=True),
                        reads=[Vb, u["Ab"]], writes=[self.psb[OB]])
            if u["last"]:
                OT, OTb = u["OT"]
                T.op("dve", lambda e: e.tensor_copy(
                    out=OT[:, 4 * pp:4 * pp + 4, :],
                    in_=self.bank(OB).rearrange("p (h t) -> p h t", t=128)),
                    reads=[self.psb[OB]], writes=[OTb])

        def out_proj(j, OTv, n_):
            OT, OTb = OTv
            for hp in range(8):
                T.op("pe", lambda e, hp=hp: e.matmul(
                    self.bank(3), lhsT=OT[:, hp, :], rhs=WO[:, hp, n_ * 512:(n_ + 1) * 512],
                    start=(hp == 0), stop=(hp == 7)), reads=[OTb, WOb], writes=[self.psb[3]])
            T.op("dve", lambda e: e.tensor_tensor(
                out=self.X[:, j, n_ * 512:(n_ + 1) * 512], in0=self.bank(3),
                in1=self.X[:, j, n_ * 512:(n_ + 1) * 512], op=ALU.add),
                reads=[self.psb[3], self.Xb[j]], writes=[self.Xb[j]])

        def qproj_slices(qb):
            chunks = [4 * qb + c for c in range(4)]
            QT, QTb = QT2[qb % 2], QTb2[qb % 2]
            gv, gb = gain5
            col0, scr0 = 4 * qb, 32 + 8 * qb
            sl = [lambda: self.rms_stats(chunks, col0, scr0)]

            def norm_chunk(i, m):
                xs, xsb = self.xsring.next()
                T.op("dve", lambda e: e.scalar_tensor_tensor(
                    out=xs, in0=self.X[:, m, :], scalar=self.stat[:, col0 + i:col0 + i + 1], in1=gv,
                    op0=ALU.mult, op1=ALU.mult), reads=[self.Xb[m], self.statb, gb], writes=[xsb])
                trv = self.bank(7).bitcast(BF16)
                for kc in range(8):
                    T.op("pe", lambda e, kc=kc: e.transpose(
                        trv[:, kc * 128:(kc + 1) * 128], xs[:, kc * 128:(kc + 1) * 128], self.ident),
                        reads=[xsb, self.identb], writes=[self.psb[7]])
                T.op("dve", lambda e: e.tensor_copy(out=HTq[:, :, i * 128:(i + 1) * 128],
                                                    in_=trv.rearrange("p (k t) -> p k t", t=128)),
                     reads=[self.psb[7]], writes=[HTqb[i]])
            for i, m in enumerate(chunks):
                sl.append(lambda i=i, m=m: norm_chunk(i, m))

            def qhead(h):
                k = h // 4
                r0 = (k % 2) * 64
                slot = (k // 2) * 4 + h % 4
                pq = self.ps[r0:r0 + 64, 7 * 512:8 * 512]
                for kc in range(8):
                    T.op("pe", lambda e, kc=kc: e.matmul(
                        pq, lhsT=WQ[:, kc, h * 64:(h + 1) * 64], rhs=HTq[:, kc, :], start=(kc == 0), stop=(kc == 7)),
                        reads=[WQb] + HTqb, writes=[self.psb[7]])
                T.op("dve", lambda e: e.tensor_scalar(
                    out=QT[r0:r0 + 64, slot, :], in0=pq, scalar1=0.125, scalar2=None, op0=ALU.mult),
                    reads=[self.psb[7]], writes=[QTb[k]])
            for h in range(16):
                sl.append(lambda h=h: qhead(h))
            return sl

        units = []
        nchain = [0]
        oproj = []
        batch_first_unit = {}
        for qb in range(4):
            batch_first_unit[qb] = len(units)
            for jl in range(4):
                j = 4 * qb + jl
                kmax = 4 * (j // 2) + (1 if j % 2 == 0 else 3)
                OTv = otr.next()
                for pp in range(2):
                    chain = dict(prev=None, ob=(6 if nchain[0] % 2 == 0 else 2))
                    nchain[0] += 1
                    for kb in range(kmax, -1, -1):
                        units.append(dict(pp=pp, kb=kb, jl=jl, j=j, kmax=kmax, first=(kb == kmax), last=(kb == 0),
                                          chain=chain, OT=OTv, QT=(QT2[qb % 2], QTb2[qb % 2]), pre=[], post=None))
                oproj.append((len(units) - 1, j, OTv))
        sl0 = qproj_slices(0)
        for f in sl0[:13]:
            f()
        late0 = sl0[13:]
        for qb in range(3):
            sl = qproj_slices(qb + 1)
            u0 = batch_first_unit[qb]
            nb = batch_first_unit[qb + 1] - u0
            assert nb >= len(sl)
            for i, f in enumerate(sl):
                units[u0 + i]["pre"].append(f)
        units[0]["pre"] = late0[:4] + units[0]["pre"]
        units[1]["pre"] = late0[4:] + units[1]["pre"]
        assert units[2]["pp"] == 1 and units[1]["pp"] == 0
        n = len(units)
        posts = {}
        for (ui, j, OTv) in oproj:
            for n_ in range(2):
                posts.setdefault(ui + 2 + 2 * n_, []).append((lambda j=j, OTv=OTv, n_=n_: out_proj(j, OTv, n_)))
        for t in range(n + 2):
            if t < n:
                for f in units[t]["pre"]:
                    f()
                stage_a(units[t])
            if 1 <= t <= n:
                stage_b(units[t - 1])
            if t >= 2:
                stage_c(units[t - 2])
                for f in posts.pop(t - 2, []):
                    f()
        for k_ in sorted(posts):
            for f in posts[k_]:
                f()
        self.phase_barrier()
        A.release()


def build(mode="FUSED", stages=None):
    P = Prog(mode)
    P.din("x", [TOK, D])
    P.din("gains", [8, D])
    P.din("ffn_win", [4, NFC, 128, 2048])
    P.din("ffn_wout", [4, 128, NFC * D])
    P.din("a_wsT", [128, 1024])
    P.din("a_bs", [1, 1024])
    P.din("a_lng", [128, 16])
    P.din("a_lnb", [128, 16])
    P.din("a_win_v", [8, 128, 2048])
    P.din("a_win_u", [16, 128, 1024])
    P.din("a_wout", [128, 16 * D])
    P.din("w_kv_k", [128, 2048])
    P.din("w_kv_v", [128, 2048])
    P.din("w_q", [128, 8 * D])
    P.din("w_o", [128, 8 * D])
    P.din("masks", [128, 4 * 512])
    P.dout("y", [TOK, D])
    es = contextlib.ExitStack()
    with es:
        P.setup(es)
        g0 = P.load_gain(0)
        P.load_x(P.dr["x"])
        st = stages or ["ffn0a", "sgu", "ffn0b", "kv", "ffn1a", "attn", "ffn1b", "final"]
        if "ffn0a" in st:
            P.ffn(0, 0, gain=g0)
        if "sgu" in st:
            P.sgu()
        if "ffn0b" in st:
            P.ffn(1, 2)
        if "kv" in st:
            P.kv_exchange()
        if "ffn1a" in st:
            P.ffn(2, 4)
        if "attn" in st:
            P.attention()
        if "ffn1b" in st and "final" in st:
            P.ffn(3, 6, final=(7, P.dr["y"]))
        else:
            if "ffn1b" in st:
                P.ffn(3, 6)
            P.store_x(P.dr["y"], gain_row=(7 if "final" in st else None))
    return P.nc


def prep_weights(inp):
    f = np.float32
    w = {}
    win = np.asarray(inp["ffn_w_in"], f).reshape(4, 8, 128, 2, NFC, 128)
    w["ffn_win"] = np.ascontiguousarray(win.transpose(0, 4, 2, 1, 3, 5)).reshape(4, NFC, 128, 2048)
    wout = np.asarray(inp["ffn_w_out"], f).reshape(4, NFC, 128, D)
    w["ffn_wout"] = np.ascontiguousarray(wout.transpose(0, 2, 1, 3)).reshape(4, 128, NFC * D)
    g = np.zeros((8, D), f)
    fn = np.asarray(inp["ffn_norm"], f)
    mn = np.asarray(inp["mix_norm"], f)
    g[0], g[1], g[2], g[3] = fn[0, 0], mn[0], fn[0, 1], np.asarray(inp["kv_norm"], f)
    g[4], g[5], g[6], g[7] = fn[1, 0], mn[1], fn[1, 1], np.asarray(inp["final_norm"], f)
    w["gains"] = g
    return w


def prep_l0(inp):
    f = np.float32
    w = {}
    w["a_wsT"] = np.ascontiguousarray(np.asarray(inp["a_w_s"], f)[0].transpose(2, 0, 1)).reshape(128, 1024)
    w["a_bs"] = np.ascontiguousarray(np.asarray(inp["a_b_s"], f)[0]).reshape(1, 1024)
    w["a_lng"] = np.ascontiguousarray(np.asarray(inp["a_ln_g"], f)[0].reshape(16, 128).T)
    w["a_lnb"] = np.ascontiguousarray(np.asarray(inp["a_ln_b"], f)[0].reshape(16, 128).T)
    awin = np.asarray(inp["a_w_in"], f)[0]
    w["a_win_v"] = np.ascontiguousarray(awin[:, 2048:].reshape(8, 128, 8, 256).transpose(2, 1, 0, 3)).reshape(8, 128, 2048)
    w["a_win_u"] = np.ascontiguousarray(awin[:, :2048].reshape(8, 128, 16, 128).transpose(2, 1, 0, 3)).reshape(16, 128, 1024)
    w["a_wout"] = np.ascontiguousarray(np.asarray(inp["a_w_out"], f)[0].reshape(16, 128, D).transpose(1, 0, 2)).reshape(128, 16 * D)
    wkv = np.asarray(inp["w_kv"], f).reshape(8, 128, 512)
    w["w_kv_k"] = np.ascontiguousarray(wkv[:, :, :256].transpose(1, 0, 2)).reshape(128, 2048)
    w["w_kv_v"] = np.ascontiguousarray(wkv[:, :, 256:].transpose(1, 0, 2)).reshape(128, 2048)
    return w


def prep_l1(inp):
    f = np.float32
    w = {}
    w["w_q"] = np.ascontiguousarray(np.asarray(inp["b_w_q"], f)[0].reshape(8, 128, D).transpose(1, 0, 2)).reshape(128, 8 * D)
    w["w_o"] = np.ascontiguousarray(np.asarray(inp["b_w_o"], f)[0].reshape(8, 128, D).transpose(1, 0, 2)).reshape(128, 8 * D)
    return w


def core_masks(h):
    s = np.arange(128)[:, None]
    t = np.arange(128)[None, :]
    diag = np.tile((s < t).astype(np.float32), (1, 4))
    ones = np.ones((128, 512), np.float32)
    zeros = np.zeros((128, 512), np.float32)
    if h == 0:
        m = [zeros, diag, diag, ones]
    else:
        m = [diag, ones, zeros, diag]
    return np.ascontiguousarray(np.concatenate(m, axis=1))


def shard_x(x):
    x = np.asarray(x, np.float32)
    out = []
    for c in range(8):
        b, h = divmod(c, 2)
        rows = np.concatenate([np.arange(local_to_global(h, j) * 128, local_to_global(h, j) * 128 + 128)
                               for j in range(NCH)])
        out.append(np.ascontiguousarray(x[b][rows]))
    return out


def unshard(ys):
    out = np.zeros((4, S, D), np.float32)
    for c in range(8):
        b, h = divmod(c, 2)
        for j in range(NCH):
            g = local_to_global(h, j)
            out[b, g * 128:(g + 1) * 128] = ys[c][j * 128:(j + 1) * 128]
    return out


def kernel(**inp):
    w = dict(prep_weights(inp), **prep_l0(inp))
    w.update(prep_l1(inp))
    xs = shard_x(inp["x"])
    nc = build()
    in_maps = [dict(w, x=xs[c], masks=core_masks(c % 2)) for c in range(8)]
    res = run_bass_kernel_spmd(nc, in_maps, core_ids=list(range(8))).results
    return unshard([r["y"] for r in res])
```
